# Optimizing a Trainium2 kernel written in Bass

```python
import math
import jax, jax.numpy as jnp
from jax import lax
import numpy as np

D_MODEL = 2048
BATCH = 8
SEQ = 2048
DEPTH = 2
DEC_BATCH = 4
DEC_SEQ = 8192
PAST_LEN = 128

D_MIX = D_MODEL
LRU_WIDTH = D_MIX // 4
LRU_BLOCKS = 8
LRU_BLOCK = LRU_WIDTH // LRU_BLOCKS
CONV_WIDTH = 4
CONV_PAD_LEFT = 2
CONV_PAD_RIGHT = 1
LRU_C = 8.0
RWKV_WIDTH = D_MIX // 4
RWKV_HEAD = 64
RWKV_HEADS = RWKV_WIDTH // RWKV_HEAD
DECAY_RANK = 64
ICLR_RANK = 64
GATE_RANK = 128
DECAY_SCALE = 0.606531
GN_EPS = 64e-5
ATTN_WIDTH = D_MIX - LRU_WIDTH - RWKV_WIDTH
HEAD_DIM = 128
N_Q_HEADS = ATTN_WIDTH // HEAD_DIM
N_KV_HEADS = 2
GQA_GROUP = N_Q_HEADS // N_KV_HEADS
KV_WIDTH = N_KV_HEADS * HEAD_DIM
WINDOW = 128
BLOCK = 128
N_BUCKETS = 32
MAX_DISTANCE = 128
D_FF = 4 * D_MODEL
EPS = 1e-6
OFF_LRU_X = 0
OFF_LRU_G = LRU_WIDTH
OFF_RWKV = 2 * LRU_WIDTH
RWKV_COLS = 3 * RWKV_WIDTH + 2 * DECAY_RANK + 2 * ICLR_RANK + GATE_RANK
OFF_ATTN = OFF_RWKV + RWKV_COLS
ATTN_COLS = ATTN_WIDTH + 2 * KV_WIDTH
D_IN = OFF_ATTN + ATTN_COLS

kernel_name = 'hymba_lru_rwkv7_swa_encoder'


def _rms_norm(x, g):
    xf = x.astype(jnp.float32)
    y = xf * lax.rsqrt(jnp.mean(xf * xf, axis=-1, keepdims=True) + EPS)
    return (y * g.astype(jnp.float32)).astype(x.dtype)


def _lin_combine(left, right):
    a_l, b_l = left
    a_r, b_r = right
    return a_l * a_r, a_r * b_l + b_r


def _rg_lru(xb, gb, conv_w, conv_b, wa, ba, wx, bx, lam):
    f32 = jnp.float32
    bsz, slen, _ = xb.shape
    xf = xb.astype(f32)
    xc = lax.conv_general_dilated(xf, conv_w.astype(f32)[:, None, :], (1,), [(CONV_PAD_LEFT, CONV_PAD_RIGHT)],
                                  dimension_numbers=('NWC', 'WIO', 'NWC'),
                                  feature_group_count=LRU_WIDTH) + conv_b.astype(f32)
    xh = xc.reshape(bsz, slen, LRU_BLOCKS, LRU_BLOCK)
    r = jax.nn.sigmoid(jnp.einsum('bsni,dnij->dbsnj', xh, wa.astype(f32)).reshape(2, bsz, slen, LRU_WIDTH)
                       + ba.astype(f32)[:, None, None, :])
    i = jax.nn.sigmoid(jnp.einsum('bsni,dnij->dbsnj', xh, wx.astype(f32)).reshape(2, bsz, slen, LRU_WIDTH)
                       + bx.astype(f32)[:, None, None, :])
    log_a = -LRU_C * r * jax.nn.softplus(-lam.astype(f32))[:, None, None, :]
    a = jnp.exp(log_a)
    mult = jnp.sqrt(-jnp.expm1(2.0 * log_a))
    pos = jnp.arange(slen)
    first = jnp.stack([pos == 0, pos == slen - 1])[:, None, :, None]
    mult = jnp.where(first, 1.0, mult)
    u = mult * i * xc[None]
    h_f = lax.associative_scan(_lin_combine, (a[0], u[0]), axis=1)[1]
    h_b = lax.associative_scan(_lin_combine, (a[1], u[1]), axis=1, reverse=True)[1]
    return (h_f + h_b) * jax.nn.gelu(gb.astype(f32), approximate=True)


def _centred_shift(z):
    zp = jnp.pad(z, ((0, 0), (1, 1), (0, 0)))
    return 0.5 * (zp[:, :-2] + zp[:, 2:])


def _heads(t):
    return t.reshape(t.shape[:-1] + (RWKV_HEADS, RWKV_HEAD))


def _rwkv_scan(r, w, kk, a, k, v, reverse):
    bsz, _, nh, n = r.shape
    xs = tuple(jnp.moveaxis(t, 1, 0) for t in (r, w, kk, a, k, v))

    def step(st, inp):
        r_t, w_t, kk_t, a_t, k_t, v_t = inp
        sa = jnp.einsum('bhvk,bhk->bhv', st, kk_t)
        st = (st * w_t[:, :, None, :] - sa[..., None] * (kk_t * a_t)[:, :, None, :]
              + v_t[..., None] * k_t[:, :, None, :])
        y = jnp.einsum('bhvk,bhk->bhv', st, r_t)
        return st, y

    s0 = jnp.zeros((bsz, nh, n, n), jnp.float32)
    _, ys = lax.scan(step, s0, xs, reverse=reverse)
    return jnp.moveaxis(ys, 0, 1)


def _rwkv7(z, mu, w0, w2, a0, a2, g2, k_k, k_a, r_k, gn_w, gn_b):
    f32 = jnp.float32
    bsz, slen, _ = z.shape
    W = RWKV_WIDTH
    z = z.astype(f32)
    z = z + (_centred_shift(z) - z) * mu.astype(f32)
    r = z[..., 0:W]
    k = z[..., W:2 * W]
    v = z[..., 2 * W:3 * W]
    o = 3 * W
    dw = z[..., o:o + 2 * DECAY_RANK].reshape(bsz, slen, 2, DECAY_RANK)
    o += 2 * DECAY_RANK
    da = z[..., o:o + 2 * ICLR_RANK].reshape(bsz, slen, 2, ICLR_RANK)
    o += 2 * ICLR_RANK
    dg = z[..., o:o + GATE_RANK]
    decay = jnp.exp(-DECAY_SCALE * jax.nn.sigmoid(
        jnp.einsum('bsdr,drc->dbsc', jnp.tanh(dw), w2.astype(f32)) + w0.astype(f32)[:, None, None, :]))
    a = jax.nn.sigmoid(jnp.einsum('bsdr,drc->dbsc', da, a2.astype(f32)) + a0.astype(f32)[:, None, None, :])
    g = jax.nn.sigmoid(dg) @ g2.astype(f32)
    kdir = k[None] * (1.0 + (a - 1.0) * k_a.astype(f32)[:, None, None, :])
    kk = _heads(k * k_k.astype(f32))
    kk = kk * lax.rsqrt(jnp.maximum(jnp.sum(kk * kk, axis=-1, keepdims=True), 1e-24))
    rh, vh = _heads(r), _heads(v)
    decay_h, a_h, kdir_h = _heads(decay), _heads(a), _heads(kdir)
    y = (_rwkv_scan(rh, decay_h[0], kk, a_h[0], kdir_h[0], vh, False)
         + _rwkv_scan(rh, decay_h[1], kk, a_h[1], kdir_h[1], vh, True))
    mean = jnp.mean(y, axis=-1, keepdims=True)
    var = jnp.mean(jnp.square(y - mean), axis=-1, keepdims=True)
    yn = ((y - mean) * lax.rsqrt(var + GN_EPS)).reshape(bsz, slen, W) * gn_w.astype(f32) + gn_b.astype(f32)
    bonus = jnp.sum(rh * (kdir_h[0] + kdir_h[1]) * r_k.astype(f32), axis=-1, keepdims=True) * vh
    return (yn + bonus.reshape(bsz, slen, W)) * g


def _t5_bucket(rel):
    half = N_BUCKETS // 2
    max_exact = half // 2
    ret = jnp.where(rel > 0, half, 0)
    n = jnp.abs(rel)
    nf = jnp.maximum(n, 1).astype(jnp.float32)
    large = max_exact + (jnp.log(nf / max_exact) / math.log(MAX_DISTANCE / max_exact)
                         * (half - max_exact)).astype(jnp.int32)
    large = jnp.minimum(large, half - 1)
    return ret + jnp.where(n < max_exact, n, large)


def _window_attention(q, k, v, sink, rel_bias):
    f32 = jnp.float32
    bsz, slen = q.shape[:2]
    nb = slen // BLOCK
    qb = q.reshape(bsz, nb, BLOCK, N_KV_HEADS, GQA_GROUP, HEAD_DIM)

    def band(t):
        tp = jnp.pad(t, ((0, 0), (BLOCK, BLOCK), (0, 0), (0, 0))).reshape(bsz, nb + 2, BLOCK, N_KV_HEADS, HEAD_DIM)
        return jnp.concatenate([tp[:, :-2], tp[:, 1:-1], tp[:, 2:]], axis=2)

    kb, vb = band(k), band(v)
    s = jnp.einsum('bnqhgd,bnkhd->bnhgqk', qb, kb).astype(f32) * (HEAD_DIM ** -0.5)
    qi = jnp.arange(BLOCK)[:, None]
    kj = jnp.arange(3 * BLOCK)[None, :]
    rel = kj - BLOCK - qi
    bias = rel_bias.astype(f32)[_t5_bucket(rel)]
    bias = jnp.transpose(bias, (2, 0, 1)).reshape(N_KV_HEADS, GQA_GROUP, BLOCK, 3 * BLOCK)
    kpos = jnp.arange(nb)[:, None] * BLOCK + kj - BLOCK
    valid = ((jnp.abs(rel) <= WINDOW)[None] & (kpos >= 0)[:, None, :] & (kpos < slen)[:, None, :])
    s = jnp.where(valid[None, :, None, None], s + bias, -jnp.inf)
    sk = sink.astype(f32).reshape(N_KV_HEADS, GQA_GROUP)[None, None, :, :, None, None]
    m = jnp.maximum(jnp.max(s, axis=-1, keepdims=True), sk)
    p = jnp.exp(s - m)
    den = jnp.sum(p, axis=-1, keepdims=True) + jnp.exp(sk - m)
    o = jnp.einsum('bnhgqk,bnkhd->bnqhgd', (p / den).astype(v.dtype), vb)
    return o.reshape(bsz, slen, ATTN_WIDTH)


def _trunk(x, norm_mix_pre, norm_mix_post, norm_ffn_pre, norm_ffn_post, w_in, w_out,
           conv_w, conv_b, lru_wa, lru_ba, lru_wx, lru_bx, lru_lambda,
           rwkv_mu, rwkv_w0, rwkv_w2, rwkv_a0, rwkv_a2, rwkv_g2, rwkv_k_k, rwkv_k_a, rwkv_r_k,
           rwkv_gn_w, rwkv_gn_b, attn_sink, rel_bias, w_up, w_down):
    bsz, slen, _ = x.shape
    for l in range(DEPTH):
        h = _rms_norm(x, norm_mix_pre[l])
        p = h @ w_in[l]
        lru_out = _rg_lru(p[..., OFF_LRU_X:OFF_LRU_G], p[..., OFF_LRU_G:OFF_RWKV], conv_w[l], conv_b[l],
                          lru_wa[l], lru_ba[l], lru_wx[l], lru_bx[l], lru_lambda[l])
        rwkv_out = _rwkv7(p[..., OFF_RWKV:OFF_ATTN], rwkv_mu[l], rwkv_w0[l], rwkv_w2[l], rwkv_a0[l], rwkv_a2[l],
                          rwkv_g2[l], rwkv_k_k[l], rwkv_k_a[l], rwkv_r_k[l], rwkv_gn_w[l], rwkv_gn_b[l])
        q = p[..., OFF_ATTN:OFF_ATTN + ATTN_WIDTH].reshape(bsz, slen, N_Q_HEADS, HEAD_DIM)
        k = p[..., OFF_ATTN + ATTN_WIDTH:OFF_ATTN + ATTN_WIDTH + KV_WIDTH].reshape(bsz, slen, N_KV_HEADS, HEAD_DIM)
        v = p[..., OFF_ATTN + ATTN_WIDTH + KV_WIDTH:D_IN].reshape(bsz, slen, N_KV_HEADS, HEAD_DIM)
        attn_out = _window_attention(q, k, v, attn_sink[l], rel_bias)
        mix = jnp.concatenate([lru_out, rwkv_out, attn_out.astype(jnp.float32)], axis=-1).astype(x.dtype) @ w_out[l]
        x = x + _rms_norm(mix, norm_mix_post[l])
        h = _rms_norm(x, norm_ffn_pre[l])
        f = jnp.square(jax.nn.relu(h @ w_up[l])) @ w_down[l]
        x = x + _rms_norm(f, norm_ffn_post[l])
    return x


def setup_inputs(seed: int = 0) -> dict:
    key = jax.random.key(seed)
    ks = jax.random.split(key, 32)
    f32 = jnp.float32
    L = DEPTH

    def nrm(k, shape, scale):
        return jax.random.normal(k, shape, f32) * scale

    def gain(k, shape):
        return 1.0 + 0.05 * jax.random.normal(k, shape, f32)

    u = jax.random.uniform(ks[14], (L, 2, LRU_WIDTH), f32, 0.9, 0.999)
    a_base = u ** (1.0 / LRU_C)
    lam = jnp.log(a_base) - jnp.log1p(-a_base)
    return {
        'x_prompt': nrm(ks[0], (BATCH, SEQ, D_MODEL), 1.0),
        'x_sample': nrm(ks[1], (DEC_BATCH, DEC_SEQ, D_MODEL), 1.0),
        'norm_mix_pre': gain(ks[2], (L, D_MODEL)),
        'norm_mix_post': gain(ks[3], (L, D_MODEL)),
        'norm_ffn_pre': gain(ks[4], (L, D_MODEL)),
        'norm_ffn_post': gain(ks[5], (L, D_MODEL)),
        'w_in': nrm(ks[6], (L, D_MODEL, D_IN), D_MODEL ** -0.5),
        'w_out': nrm(ks[7], (L, D_MIX, D_MODEL), D_MIX ** -0.5),
        'conv_w': nrm(ks[8], (L, CONV_WIDTH, LRU_WIDTH), CONV_WIDTH ** -0.5),
        'conv_b': nrm(ks[9], (L, LRU_WIDTH), 0.01),
        'lru_wa': nrm(ks[10], (L, 2, LRU_BLOCKS, LRU_BLOCK, LRU_BLOCK), LRU_BLOCK ** -0.5),
        'lru_ba': nrm(ks[11], (L, 2, LRU_WIDTH), 0.01),
        'lru_wx': nrm(ks[12], (L, 2, LRU_BLOCKS, LRU_BLOCK, LRU_BLOCK), LRU_BLOCK ** -0.5),
        'lru_bx': nrm(ks[13], (L, 2, LRU_WIDTH), 0.01),
        'lru_lambda': lam,
        'rwkv_mu': jax.random.uniform(ks[15], (L, RWKV_COLS), f32, 0.0, 1.0),
        'rwkv_w0': jax.random.uniform(ks[16], (L, 2, RWKV_WIDTH), f32, -2.0, 2.0),
        'rwkv_w2': nrm(ks[17], (L, 2, DECAY_RANK, RWKV_WIDTH), 0.5 * DECAY_RANK ** -0.5),
        'rwkv_a0': nrm(ks[18], (L, 2, RWKV_WIDTH), 0.5),
        'rwkv_a2': nrm(ks[19], (L, 2, ICLR_RANK, RWKV_WIDTH), 0.5 * ICLR_RANK ** -0.5),
        'rwkv_g2': nrm(ks[20], (L, GATE_RANK, RWKV_WIDTH), GATE_RANK ** -0.5),
        'rwkv_k_k': 0.85 + 0.05 * jax.random.normal(ks[21], (L, RWKV_WIDTH), f32),
        'rwkv_k_a': gain(ks[22], (L, 2, RWKV_WIDTH)),
        'rwkv_r_k': nrm(ks[23], (L, RWKV_HEADS, RWKV_HEAD), 0.1),
        'rwkv_gn_w': gain(ks[24], (L, RWKV_WIDTH)),
        'rwkv_gn_b': nrm(ks[25], (L, RWKV_WIDTH), 0.01),
        'attn_sink': nrm(ks[26], (L, N_Q_HEADS), 0.5),
        'rel_bias': nrm(ks[27], (N_BUCKETS, N_Q_HEADS), 0.5),
        'w_up': nrm(ks[28], (L, D_MODEL, D_FF), D_MODEL ** -0.5),
        'w_down': nrm(ks[29], (L, D_FF, D_MODEL), D_FF ** -0.5),
    }


def reference(x_prompt, x_sample, norm_mix_pre, norm_mix_post, norm_ffn_pre, norm_ffn_post, w_in, w_out,
              conv_w, conv_b, lru_wa, lru_ba, lru_wx, lru_bx, lru_lambda,
              rwkv_mu, rwkv_w0, rwkv_w2, rwkv_a0, rwkv_a2, rwkv_g2, rwkv_k_k, rwkv_k_a, rwkv_r_k,
              rwkv_gn_w, rwkv_gn_b, attn_sink, rel_bias, w_up, w_down):
    weights = (norm_mix_pre, norm_mix_post, norm_ffn_pre, norm_ffn_post, w_in, w_out,
               conv_w, conv_b, lru_wa, lru_ba, lru_wx, lru_bx, lru_lambda,
               rwkv_mu, rwkv_w0, rwkv_w2, rwkv_a0, rwkv_a2, rwkv_g2, rwkv_k_k, rwkv_k_a, rwkv_r_k,
               rwkv_gn_w, rwkv_gn_b, attn_sink, rel_bias, w_up, w_down)
    y_prompt = _trunk(x_prompt, *weights)
    y_sample = _trunk(x_sample, *weights)
    return (y_prompt, y_sample)
```

```python
import math
import numpy as np
from contextlib import ExitStack
import concourse.bass as bass
import concourse.mybir as mybir
from concourse.bass_utils import run_bass_kernel_spmd

F32 = mybir.dt.float32
BF16 = mybir.dt.bfloat16
AF = mybir.ActivationFunctionType
ALU = mybir.AluOpType

D = 2048; DIN = 4480; DFF = 8192; NL = 2
EPS = 1e-6; GN_EPS = 64e-5; DECAY_SCALE = 0.606531
GC1 = math.sqrt(2.0 / math.pi); GC2 = GC1 * 0.044715


class Buf:
    __slots__ = ("t", "name", "lw", "rd", "sem", "cnt", "strict", "tiny")

    def __init__(self, t, name, strict=False):
        self.t = t; self.name = name; self.lw = None; self.rd = {}; self.sem = None; self.cnt = 0
        self.strict = strict; self.tiny = False

    def __getitem__(self, idx):
        return self.t[idx]


ENGS = {"pe": "tensor", "act": "scalar", "dve": "vector", "pool": "gpsimd", "sp": "sync"}


class Prog:
    def __init__(self, nc, es):
        self.nc = nc; self.es = es
        self.eng = {k: getattr(nc, v) for k, v in ENGS.items()}
        self.sem = {k: es.enter_context(nc.semaphore("sem_" + k)) for k in ENGS}
        self.cnt = {k: 0 for k in ENGS}
        self.waited = {k: {} for k in ENGS}
        self.dh = []; self.dc = []; self.free_slots = []; self.scope_slots = [[]]
        self.scopes = [es]
        self.uid = 0

    def push(self):
        s = ExitStack(); self.scopes.append(s); self.scope_slots.append([]); return s

    def pop(self):
        self.barrier()
        self.free_slots.extend(self.scope_slots.pop())
        self.scopes.pop().close()

    def sbuf(self, name, shape, dt):
        self.uid += 1
        t = self.scopes[-1].enter_context(self.nc.sbuf_tensor("%s_%d" % (name, self.uid), list(shape), dt))
        n = 1
        for d_ in shape[1:]:
            n *= d_
        return Buf(t, name, strict=(n <= 64))

    def psum(self, name, shape, dt):
        self.uid += 1
        t = self.scopes[-1].enter_context(self.nc.psum_tensor("%s_%d" % (name, self.uid), list(shape), dt))
        return Buf(t, name)

    def dram(self, name, shape, dt, kind="Internal"):
        if kind == "Internal" and getattr(self, "dbg", False) and not name.startswith("wb"):
            kind = "ExternalOutput"
        t = self.nc.dram_tensor(name, list(shape), dt, kind=kind)
        return Buf(t.ap(), name)

    def _semof(self, key):
        if key[0] == "e":
            return self.sem[key[1]]
        return self.dh[key[1]]

    def _deps(self, eng, reads, writes):
        deps = {}
        me = ("e", eng)
        selfv = 0
        for b in reads:
            if b.lw is not None:
                k, v = b.lw
                if deps.get(k, 0) < v: deps[k] = v
                if (b.strict or b.tiny) and k == me and v > selfv: selfv = v
        for b in writes:
            if b.lw is not None:
                k, v = b.lw
                if deps.get(k, 0) < v: deps[k] = v
                if (b.strict or b.tiny) and k == me and v > selfv: selfv = v
            for k, v in b.rd.items():
                if deps.get(k, 0) < v: deps[k] = v
                if (b.strict or b.tiny) and k == me and v > selfv: selfv = v
        w = self.waited[eng]
        e = self.eng[eng]
        for k, v in deps.items():
            if k == me:
                if selfv == 0 or w.get(k, 0) >= selfv: continue
                w[k] = selfv
                e.wait_ge(self.sem[eng], selfv)
                continue
            if w.get(k, 0) >= v: continue
            w[k] = v
            e.wait_ge(self._semof(k), v)

    def _mark(self, tok, reads, writes):
        for b in writes:
            b.lw = tok; b.rd = {}
        for b in reads:
            if b.lw is not tok:
                b.rd[tok[0]] = tok[1]

    def op(self, eng, fn, reads=(), writes=(), tiny=False):
        self._deps(eng, reads, writes)
        self.cnt[eng] += 1
        tok = (("e", eng), self.cnt[eng])
        fn(self.eng[eng]).then_inc(self.sem[eng], 1)
        self._mark(tok, reads, writes)
        for b in writes:
            b.tiny = tiny

    def dma(self, q, out_ap, in_ap, sb, reads=(), writes=(), **kw):
        self._deps(q, reads, writes)
        if sb.sem is None:
            if self.free_slots:
                sb.sem = self.free_slots.pop()
            else:
                self.dh.append(self.es.enter_context(self.nc.semaphore("d%d" % len(self.dh))))
                self.dc.append(0)
                sb.sem = len(self.dh) - 1
            self.scope_slots[-1].append(sb.sem)
        i = sb.sem
        self.dc[i] += 16
        tok = (("d", i), self.dc[i])
        self.eng[q].dma_start(out=out_ap, in_=in_ap, **kw).then_inc(self.dh[i], 16)
        self._mark(tok, reads, writes)

    def barrier(self):
        for eng in ENGS:
            w = self.waited[eng]; e = self.eng[eng]
            for o in ENGS:
                if o == eng: continue
                k = ("e", o); v = self.cnt[o]
                if v > w.get(k, 0):
                    w[k] = v; e.wait_ge(self.sem[o], v)
            for i in range(len(self.dh)):
                k = ("d", i)
                if self.dc[i] > w.get(k, 0):
                    w[k] = self.dc[i]; e.wait_ge(self.dh[i], self.dc[i])

    def ts(self, eng, out, in0, s1, op0, s2=None, op1=None, R=(), W=(), tiny=False):
        if op1 is None:
            self.op(eng, lambda e: e.tensor_scalar(out=out, in0=in0, scalar1=s1, scalar2=None, op0=op0), R, W, tiny=tiny)
        else:
            self.op(eng, lambda e: e.tensor_scalar(out=out, in0=in0, scalar1=s1, scalar2=s2, op0=op0, op1=op1), R, W, tiny=tiny)

    def tt(self, eng, out, in0, in1, op, R=(), W=()):
        self.op(eng, lambda e: e.tensor_tensor(out=out, in0=in0, in1=in1, op=op), R, W)

    def stt(self, out, in0, s, in1, op0, op1, R=(), W=()):
        self.op("dve", lambda e: e.scalar_tensor_tensor(out=out, in0=in0, scalar=s, in1=in1, op0=op0, op1=op1), R, W)

    def act(self, out, in_, func, bias=None, scale=None, accum=None, R=(), W=()):
        kw = {}
        if bias is not None: kw["bias"] = bias
        if scale is not None: kw["scale"] = scale
        if accum is not None: kw["accum_out"] = accum
        self.op("act", lambda e: e.activation(out=out, in_=in_, func=func, **kw), R, W)

    def copy(self, eng, out, in_, R=(), W=()):
        if eng == "act":
            self.act(out, in_, AF.Copy, R=R, W=W)
        else:
            self.op(eng, lambda e: e.tensor_copy(out=out, in_=in_), R, W)

    def mm(self, out, lhsT, rhs, start=True, stop=True, R=(), W=()):
        self.op("pe", lambda e: e.matmul(out, lhsT=lhsT, rhs=rhs, start=start, stop=stop), R, W)

    def tr(self, out, in_, ident, R=(), W=()):
        self.op("pe", lambda e: e.transpose(out=out, in_=in_, identity=ident), R, W)


class PsumPool:
    def __init__(self, P):
        self.f = [P.psum("psf%d" % i, [128, 512], F32) for i in range(6)]
        self.b = [P.psum("psb%d" % i, [128, 1024], BF16) for i in range(2)]
        self.i = 0; self.j = 0

    def f32(self):
        self.i += 1
        return self.f[self.i % 6]

    def bf(self):
        self.j += 1
        return self.b[self.j % 2]


def build(T, dbg=False, nl=NL):
    SEG = T // 4
    NB = T // 128
    NBSEG = SEG // 128
    nc = bass.Bass("TRN2", target_bir_lowering=False)
    es = ExitStack()
    P = Prog(nc, es)
    P.dbg = dbg

    def din(name, shape, dt=F32):
        return P.dram(name, shape, dt, kind="ExternalInput")

    x_in = din("x", [T, D])
    lk_in = din("lk", [128, 4])
    w_in = din("w_in", [NL, D, DIN]); w_out = din("w_out", [NL, D, D])
    w_up = din("w_up", [NL, D, DFF]); w_dn = din("w_down", [NL, DFF, D])
    gpre1 = din("gpre1", [NL, 128, 16]); gpre2 = din("gpre2", [NL, 128, 16])
    gpost1 = din("gpost1", [NL, 128, D]); gpost2 = din("gpost2", [NL, 128, D])
    lru_cols = din("lru_cols", [NL, 128, 48])
    lru_bd = din("lru_bd", [NL, 16, 128, 128])
    rk_mu = din("rk_mu", [NL, 64, 30])
    rk_cols = din("rk_cols", [NL, 64, 80])
    rk_w2 = din("rk_w2", [NL, 2, 64, 512]); rk_a2 = din("rk_a2", [NL, 2, 64, 512])
    rk_g2 = din("rk_g2", [NL, 64, 2, 512])
    sink_in = din("sink", [NL, 128, 8])
    relb = din("relb", [32, 8])
    oh_in = din("oh", [33, 768])
    y_out = P.dram("y", [T, D], F32, kind="ExternalOutput")

    wb_in = [P.dram("wbin%d" % l, [D, DIN], BF16) for l in range(NL)]
    wb_out = [P.dram("wbout%d" % l, [D, D], BF16) for l in range(NL)]
    wb_up = [P.dram("wbup%d" % l, [D, DFF], BF16) for l in range(NL)]
    wb_dn = [P.dram("wbdn%d" % l, [DFF, D], BF16) for l in range(NL)]
    xs1 = P.dram("xs1", [T, D], F32)
    lrux = P.dram("lrux", [512, T], F32); lrug = P.dram("lrug", [512, T], BF16)
    zsc = P.dram("zsc", [1920, T], F32)
    qT = P.dram("qT", [1024, T], BF16); kT = P.dram("kT", [256, T], BF16); vtm = P.dram("vtm", [T, 256], BF16)
    mixT = P.dram("mixT", [D, T], BF16)
    hfs = P.dram("hfs", [512, T], F32); yfs = P.dram("yfs", [512, T], F32)
    btab = P.dram("btab", [8, 768], F32)

    PS = PsumPool(P)
    identf = P.sbuf("identf", [128, 128], F32)
    ident = P.sbuf("ident", [128, 128], BF16)
    ones = P.sbuf("ones", [128, 128], BF16)
    ones64m = P.sbuf("ones64m", [64, 64], BF16)
    maskA = P.sbuf("maskA", [128, 256], F32)
    maskL = P.sbuf("maskL", [128, 128], F32)
    onesf = P.sbuf("onesf", [128, 128], F32)
    lk = P.sbuf("lk", [128, 4], F32)
    biasT = P.sbuf("biasT", [128, 8, 384], F32)
    CONSTS = [identf, ident, ones, ones64m, maskA, maskL, onesf, lk, biasT]

    P.op("pool", lambda e: e.memset(identf[:], 1.0), [], [identf])
    P.op("pool", lambda e: e.affine_select(out=identf[:], in_=identf[:], pattern=[[-1, 128]], compare_op=ALU.is_equal,
                                           fill=0.0, base=0, channel_multiplier=1), [identf], [identf])
    P.copy("dve", ident[:], identf[:], [identf], [ident])
    P.op("pool", lambda e: e.memset(onesf[:], 1.0), [], [onesf])
    P.copy("dve", ones[:], onesf[:], [onesf], [ones])
    P.ts("dve", ones64m[:], onesf[0:64, 0:64], 1.0 / 64, ALU.mult, R=[onesf], W=[ones64m])
    P.op("pool", lambda e: e.memset(maskA[:], 1.0), [], [maskA])
    P.op("pool", lambda e: e.memset(maskL[:], 1.0), [], [maskL])
    P.op("pool", lambda e: e.affine_select(out=maskA[:, 0:128], in_=maskA[:, 0:128], pattern=[[1, 128]], compare_op=ALU.is_gt,
                                           fill=0.0, base=0, channel_multiplier=-1), [maskA], [maskA])
    P.op("pool", lambda e: e.affine_select(out=maskA[:, 128:256], in_=maskA[:, 128:256], pattern=[[1, 128]], compare_op=ALU.is_ge,
                                           fill=0.0, base=0, channel_multiplier=-1), [maskA], [maskA])
    P.op("pool", lambda e: e.affine_select(out=maskL[:], in_=maskL[:], pattern=[[-1, 128]], compare_op=ALU.is_gt,
                                           fill=0.0, base=0, channel_multiplier=1), [maskL], [maskL])
    P.dma("sp", lk[:], lk_in[:, :], lk, writes=[lk])
    LINK = lk[:, 0:1]; OML = lk[:, 1:2]; NEGM = lk[:, 2:3]

    P.push()
    rbx = P.sbuf("rbx", [33, 8], F32); ohs = P.sbuf("ohs", [33, 768], F32)
    bts = P.sbuf("bts", [8, 768], F32); tz = P.sbuf("tz", [128, 8, 512], F32)
    P.op("pool", lambda e: e.memset(rbx[:], -30000.0), [], [rbx])
    P.dma("sp", rbx[0:32, :], relb[:, :], rbx, writes=[rbx])
    P.dma("sp", ohs[:], oh_in[:, :], ohs, writes=[ohs])
    for c in range(2):
        pb = PS.f32()
        P.mm(pb[0:8, 0:384], rbx[:, :], ohs[:, c * 384:(c + 1) * 384], R=[rbx, ohs], W=[pb])
        P.copy("act", bts[:, c * 384:(c + 1) * 384], pb[0:8, 0:384], [pb], [bts])
    P.dma("sp", btab[:, :], bts[:], bts, reads=[bts], writes=[btab])
    for h in range(8):
        src = bass.AP(tensor=btab.t.tensor, offset=h * 768 + 128, ap=[[1, 128], [1, 512]])
        P.dma("sp", tz[:, h, :], src, tz, reads=[btab], writes=[tz])
    for h in range(8):
        for j in (-1, 0, 1):
            hi = 256 + 128 * j
            P.copy("dve", biasT[:, h, (j + 1) * 128:(j + 2) * 128], tz[:, h, hi:hi - 128:-1], [tz], [biasT])
    P.pop()

    P.push()
    s32 = [P.sbuf("s32", [128, 2048], F32) for _ in range(3)]
    s16 = [P.sbuf("s16", [128, 2048], BF16) for _ in range(3)]
    ci = 0
    for l in range(NL):
        for (src, dst, R_, C_) in ((w_in, wb_in, D, DIN), (w_out, wb_out, D, D), (w_up, wb_up, D, DFF), (w_dn, wb_dn, DFF, D)):
            for r0 in range(0, R_, 128):
                for c0 in range(0, C_, 2048):
                    cw = min(2048, C_ - c0)
                    a = s32[ci % 3]; b = s16[ci % 3]
                    P.dma("sp", a[:, 0:cw], src[l, r0:r0 + 128, c0:c0 + cw], a, writes=[a])
                    P.copy(("act", "dve", "pool")[ci % 3], b[:, 0:cw], a[:, 0:cw], [a], [b])
                    P.dma("sp" if ci % 2 else "pool", dst[l][r0:r0 + 128, c0:c0 + cw], b[:, 0:cw], b, reads=[b])
                    ci += 1
    P.pop()

    def rstd_from_ssq(ssq, tmp, rstd, n):
        P.ts("pool", tmp, ssq, 1.0 / n, ALU.mult, EPS, ALU.add, R=[SM], W=[SM])
        P.act(tmp, tmp, AF.Sqrt, R=[SM], W=[SM])
        P.op("dve", lambda e: e.reciprocal(out=rstd, in_=tmp), [SM], [SM])

    SM = P.sbuf("small", [128, 16], F32)
    CONSTS.append(SM)

    xcur = x_in
    for l in range(nl):
        xnext = xs1 if l < nl - 1 else y_out
        P.push()
        MTA = min(512, SEG)
        NSUB = MTA // 128
        xa = [P.sbuf("xa", [128, D], F32) for _ in range(2)]
        xn = [P.sbuf("xn", [128, D], BF16) for _ in range(2)]
        hT = [P.sbuf("hT", [128, 16, MTA], BF16) for _ in range(2)]
        wb = [P.sbuf("wbA", [128, 16, 640], BF16) for _ in range(2)]
        stf = [P.sbuf("stf", [128, 512], F32) for _ in range(4)]
        stb = [P.sbuf("stb", [128, 512], BF16) for _ in range(4)]
        gt1 = P.sbuf("gt1", [128, 512], F32); gt2 = P.sbuf("gt2", [128, 512], F32)
        gcol = P.sbuf("gcol", [128, 16], F32)
        P.dma("sp", gcol[:], gpre1[l, :, :], gcol, writes=[gcol])
        wv = wb_in[l].t.rearrange("(k p) c -> p k c", p=128)
        si = 0; wi = 0
        for m in range(T // MTA):
            h = hT[m % 2]
            for sub in range(NSUB):
                a = xa[sub % 2]; b = xn[sub % 2]
                t0 = m * MTA + sub * 128
                P.dma("sp", a[:], xcur[t0:t0 + 128, :], a, writes=[a])
                P.act(b[:], a[:], AF.Square, R=[a], W=[b])
                P.op("dve", lambda e, junk=b: e.reduce_sum(out=SM[:, 0:1], in_=junk[:], axis=mybir.AxisListType.X), [b], [SM])
                rstd_from_ssq(SM[:, 0:1], SM[:, 1:2], SM[:, 2:3], D)
                P.ts("dve", b[:], a[:], SM[:, 2:3], ALU.mult, R=[a, SM], W=[b])
                for kq in range(4):
                    pt = PS.bf()
                    for kk in range(4):
                        k = kq * 4 + kk
                        P.tr(pt[:, kk * 128:(kk + 1) * 128], b[:, k * 128:(k + 1) * 128], ident[:], [b, ident], [pt])
                    for kk in range(4):
                        k = kq * 4 + kk
                        eng = "dve" if kk % 2 else "pool"
                        if eng == "pool":
                            P.act(h[:, k, sub * 128:(sub + 1) * 128], pt[:, kk * 128:(kk + 1) * 128], AF.Copy, scale=gcol[:, k:k + 1], R=[pt, gcol], W=[h])
                        else:
                            P.ts("dve", h[:, k, sub * 128:(sub + 1) * 128], pt[:, kk * 128:(kk + 1) * 128], gcol[:, k:k + 1], ALU.mult, R=[pt, gcol], W=[h])
            for cg in range(7):
                w = wb[wi % 2]; wi += 1
                P.dma("sp", w[:], wv[:, :, cg * 640:(cg + 1) * 640], w, writes=[w])
                for cc in range(5):
                    c = cg * 5 + cc
                    if c >= 33:
                        continue
                    ps = PS.f32()
                    for k in range(16):
                        P.mm(ps[:, 0:MTA], w[:, k, cc * 128:(cc + 1) * 128], h[:, k, :], start=(k == 0), stop=(k == 15), R=[w, h], W=[ps])
                    tsl = slice(m * MTA, (m + 1) * MTA)
                    sf = stf[si % 4]; sb_ = stb[si % 4]; si += 1
                    if c < 4:
                        P.copy("act", sf[:, 0:MTA], ps[:, 0:MTA], [ps], [sf])
                        P.dma("pool", lrux[c * 128:(c + 1) * 128, tsl], sf[:, 0:MTA], sf, reads=[sf])
                    elif c < 8:
                        P.act(gt1[:, 0:MTA], ps[:, 0:MTA], AF.Square, R=[ps], W=[gt1])
                        P.ts("dve", gt1[:, 0:MTA], gt1[:, 0:MTA], 2 * GC2, ALU.mult, 2 * GC1, ALU.add, R=[gt1], W=[gt1])
                        P.tt("dve", gt2[:, 0:MTA], gt1[:, 0:MTA], ps[:, 0:MTA], ALU.mult, R=[gt1, ps], W=[gt2])
                        P.act(gt2[:, 0:MTA], gt2[:, 0:MTA], AF.Sigmoid, R=[gt2], W=[gt2])
                        P.tt("dve", sb_[:, 0:MTA], gt2[:, 0:MTA], ps[:, 0:MTA], ALU.mult, R=[gt2, ps], W=[sb_])
                        P.dma("pool", lrug[(c - 4) * 128:(c - 3) * 128, tsl], sb_[:, 0:MTA], sb_, reads=[sb_])
                    elif c < 23:
                        P.copy("act" if c % 2 else "dve", sf[:, 0:MTA], ps[:, 0:MTA], [ps], [sf])
                        P.dma("pool", zsc[(c - 8) * 128:(c - 7) * 128, tsl], sf[:, 0:MTA], sf, reads=[sf])
                    elif c < 31:
                        P.act(sb_[:, 0:MTA], ps[:, 0:MTA], AF.Copy, scale=128.0 ** -0.5, R=[ps], W=[sb_])
                        P.dma("pool", qT[(c - 23) * 128:(c - 22) * 128, tsl], sb_[:, 0:MTA], sb_, reads=[sb_])
                    else:
                        P.copy("dve", sb_[:, 0:MTA], ps[:, 0:MTA], [ps], [sb_])
                        P.dma("pool", kT[(c - 31) * 128:(c - 30) * 128, tsl], sb_[:, 0:MTA], sb_, reads=[sb_])
                if cg == 6:
                    for sub in range(NSUB):
                        ps = PS.f32()
                        for k in range(16):
                            P.mm(ps[:, 0:256], h[:, k, sub * 128:(sub + 1) * 128], w[:, k, 384:640], start=(k == 0), stop=(k == 15), R=[w, h], W=[ps])
                        sb_ = stb[si % 4]; si += 1
                        P.copy("act", sb_[:, 0:256], ps[:, 0:256], [ps], [sb_])
                        t0 = m * MTA + sub * 128
                        P.dma("pool", vtm[t0:t0 + 128, :], sb_[:, 0:256], sb_, reads=[sb_])
        P.pop()

        P.push()
        lc = P.sbuf("lc", [128, 48], F32)
        cc_ = P.sbuf("cc", [128, 16], F32)
        bdf = P.sbuf("bdf", [128, 16, 128], F32); bdb = P.sbuf("bdb", [128, 16, 128], BF16)
        P.dma("sp", lc[:], lru_cols[l, :, :], lc, writes=[lc])
        P.dma("sp", bdf[:], lru_bd[l].rearrange("g p c -> p g c"), bdf, writes=[bdf])
        P.copy("dve", bdb[:], bdf[:], [bdf], [bdb])
        nw = P.sbuf("nw", [128, 48], F32)
        zc = nw[:, 0:8]; tcur = nw[:, 8:16]; th = nw[:, 16:24]; num = nw[:, 24:32]; dn = nw[:, 32:40]
        P.act(zc, lc[:, 36:44], AF.Exp, scale=-1.0, R=[lc], W=[nw])
        P.ts("pool", dn, zc, 2.0, ALU.add, R=[nw], W=[nw])
        P.op("dve", lambda e: e.reciprocal(out=dn, in_=dn), [nw], [nw])
        P.tt("pool", zc, zc, dn, ALU.mult, R=[nw], W=[nw])
        P.copy("dve", tcur, zc, [nw], [nw])
        for _ in range(4):
            P.act(th, tcur, AF.Tanh, R=[nw], W=[nw])
            P.tt("pool", num, th, zc, ALU.subtract, R=[nw], W=[nw])
            P.tt("dve", dn, th, th, ALU.mult, R=[nw], W=[nw])
            P.ts("pool", dn, dn, -1.0, ALU.mult, 1.0, ALU.add, R=[nw], W=[nw])
            P.op("dve", lambda e: e.reciprocal(out=dn, in_=dn), [nw], [nw])
            P.tt("pool", num, num, dn, ALU.mult, R=[nw], W=[nw])
            P.tt("dve", tcur, tcur, num, ALU.subtract, R=[nw], W=[nw])
        P.ts("pool", cc_[:, 8:16], tcur, -32.0, ALU.mult, R=[nw], W=[cc_])
        P.ts("pool", cc_[:, 0:8], tcur, -16.0, ALU.mult, R=[nw], W=[cc_])
        xp = P.sbuf("xp", [128, SEG + 3], F32); xc = P.sbuf("xc", [128, SEG], F32); xcb = P.sbuf("xcb", [128, SEG], BF16)
        rt = P.sbuf("rt", [128, SEG], F32); it = P.sbuf("it", [128, SEG], F32); at = P.sbuf("at", [128, SEG], F32)
        mt = P.sbuf("mt", [128, SEG], F32); ht = P.sbuf("ht", [128, SEG], F32); hfb = P.sbuf("hfb", [128, SEG], F32)
        ggb = P.sbuf("ggb", [128, SEG], BF16); ob = P.sbuf("ob", [128, SEG], BF16)
        carry = P.sbuf("carry", [128, 4], F32)
        CW = min(512, SEG)
        for d in range(2):
            segs = range(4) if d == 0 else range(3, -1, -1)
            for si_, seg in enumerate(segs):
                for ct in range(4):
                    rows = slice(ct * 128, (ct + 1) * 128)
                    s0 = seg * SEG
                    P.dma("sp", xp[:, 2:2 + SEG], lrux[rows, s0:s0 + SEG], xp, writes=[xp])
                    if seg > 0:
                        P.dma("sp", xp[:, 0:2], lrux[rows, s0 - 2:s0], xp, writes=[xp])
                        P.ts("dve", xp[:, 0:2], xp[:, 0:2], LINK, ALU.mult, R=[xp, lk], W=[xp], tiny=True)
                    else:
                        P.op("dve", lambda e: e.memset(xp[:, 0:2], 0.0), [], [xp], tiny=True)
                    if seg < 3:
                        P.dma("sp", xp[:, SEG + 2:SEG + 3], lrux[rows, s0 + SEG:s0 + SEG + 1], xp, writes=[xp], allow_slow_non_contiguous=True)
                        P.ts("dve", xp[:, SEG + 2:SEG + 3], xp[:, SEG + 2:SEG + 3], LINK, ALU.mult, R=[xp, lk], W=[xp], tiny=True)
                    else:
                        P.op("dve", lambda e: e.memset(xp[:, SEG + 2:SEG + 3], 0.0), [], [xp], tiny=True)
                    cw = lambda tap: lc[:, ct * 4 + tap:ct * 4 + tap + 1]
                    P.ts("dve", xc[:], xp[:, 0:SEG], cw(0), ALU.mult, lc[:, 16 + ct:17 + ct], ALU.add, R=[xp, lc], W=[xc])
                    for tap in (1, 2, 3):
                        P.stt(xc[:], xp[:, tap:tap + SEG], cw(tap), xc[:], ALU.mult, ALU.add, R=[xp, lc, xc], W=[xc])
                    P.copy("pool", xcb[:], xc[:], [xc], [xcb])
                    for gate, dst in ((0, rt), (1, it)):
                        gi = gate * 8 + d * 4 + ct
                        bcol = lc[:, 20 + gate * 8 + d * 4 + ct:21 + gate * 8 + d * 4 + ct]
                        for c0 in range(0, SEG, CW):
                            ps = PS.f32()
                            P.mm(ps[:, 0:CW], bdb[:, gi, :], xcb[:, c0:c0 + CW], R=[bdb, xcb], W=[ps])
                            P.act(dst[:, c0:c0 + CW], ps[:, 0:CW], AF.Sigmoid, bias=bcol, R=[ps, lc], W=[dst])
                    ccol = cc_[:, d * 4 + ct:d * 4 + ct + 1]; c2col = cc_[:, 8 + d * 4 + ct:9 + d * 4 + ct]
                    P.act(at[:], rt[:], AF.Exp, scale=ccol, R=[rt, cc_], W=[at])
                    P.act(mt[:], rt[:], AF.Exp, scale=c2col, R=[rt, cc_], W=[mt])
                    P.act(hfb[:], rt[:], AF.Tanh, scale=ccol, R=[rt, cc_], W=[hfb])
                    P.stt(mt[:], mt[:], 1.0, hfb[:], ALU.add, ALU.mult, R=[mt, hfb], W=[mt])
                    P.act(mt[:], mt[:], AF.Sqrt, scale=-1.0, R=[mt], W=[mt])
                    fc = 0 if d == 0 else SEG - 1
                    inner = (seg > 0) if d == 0 else (seg < 3)
                    if inner:
                        P.ts("dve", mt[:, fc:fc + 1], mt[:, fc:fc + 1], LINK, ALU.mult, OML, ALU.add, R=[mt, lk], W=[mt], tiny=True)
                    else:
                        P.op("dve", lambda e: e.memset(mt[:, fc:fc + 1], 1.0), [], [mt], tiny=True)
                    P.tt("dve", it[:], it[:], mt[:], ALU.mult, R=[it, mt], W=[it])
                    P.tt("pool", it[:], it[:], xc[:], ALU.mult, R=[it, xc], W=[it])
                    init = carry[:, ct:ct + 1] if inner else 0.0
                    if d == 0:
                        P.op("dve", lambda e, init=init: e.tensor_tensor_scan(out=ht[:], data0=at[:], data1=it[:], initial=init, op0=ALU.mult, op1=ALU.add), [at, it, carry], [ht])
                        P.ts("dve", carry[:, ct:ct + 1], ht[:, SEG - 1:SEG], LINK, ALU.mult, R=[ht, lk], W=[carry])
                        P.dma("pool", hfs[rows, s0:s0 + SEG], ht[:], ht, reads=[ht])
                    else:
                        P.op("dve", lambda e, init=init: e.tensor_tensor_scan(out=ht[:, ::-1], data0=at[:, ::-1], data1=it[:, ::-1], initial=init, op0=ALU.mult, op1=ALU.add), [at, it, carry], [ht])
                        P.ts("dve", carry[:, ct:ct + 1], ht[:, 0:1], LINK, ALU.mult, R=[ht, lk], W=[carry])
                        P.dma("sp", hfb[:], hfs[rows, s0:s0 + SEG], hfb, reads=[hfs], writes=[hfb])
                        P.dma("sp", ggb[:], lrug[rows, s0:s0 + SEG], ggb, reads=[lrug], writes=[ggb])
                        P.tt("pool", ht[:], ht[:], hfb[:], ALU.add, R=[ht, hfb], W=[ht])
                        P.tt("dve", ob[:], ht[:], ggb[:], ALU.mult, R=[ht, ggb], W=[ob])
                        P.dma("pool", mixT[rows, s0:s0 + SEG], ob[:], ob, reads=[ob])
            if d == 0:
                P.barrier()
        P.pop()

        P.push()
        SL = min(512, SEG)
        NCH = SL // 128
        rc = P.sbuf("rc", [64, 80], F32); rmu = P.sbuf("rmu", [64, 30], F32); omka = P.sbuf("omka", [64, 16], F32)
        P.dma("sp", rc[:], rk_cols[l, :, :], rc, writes=[rc])
        P.dma("sp", rmu[:], rk_mu[l, :, :], rmu, writes=[rmu])
        P.ts("dve", omka[:], rc[:, 32:48], -1.0, ALU.mult, 1.0, ALU.add, R=[rc], W=[omka])
        w2f = P.sbuf("w2f", [64, 2, 512], F32); w2b = P.sbuf("w2b", [64, 2, 512], BF16)
        a2f = P.sbuf("a2f", [64, 2, 512], F32); a2b = P.sbuf("a2b", [64, 2, 512], BF16)
        g2f = P.sbuf("g2f", [64, 2, 512], F32); g2b = P.sbuf("g2b", [64, 2, 512], BF16)
        P.dma("sp", w2f[:], rk_w2[l].rearrange("d r c -> r d c"), w2f, writes=[w2f])
        P.dma("sp", a2f[:], rk_a2[l].rearrange("d r c -> r d c"), a2f, writes=[a2f])
        P.dma("sp", g2f[:], rk_g2[l, :, :, :], g2f, writes=[g2f])
        P.copy("dve", w2b[:], w2f[:], [w2f], [w2b]); P.copy("dve", a2b[:], a2f[:], [a2f], [a2b]); P.copy("dve", g2b[:], g2f[:], [g2f], [g2b])
        zp = [P.sbuf("zp", [64, SL + 2], F32) for _ in range(2)]
        tmp1 = P.sbuf("tmp1", [64, SL], F32); tmp2 = P.sbuf("tmp2", [64, SL], F32)
        zlo = [P.sbuf("zlo", [64, SL], BF16) for _ in range(5)]
        zr = P.sbuf("zr", [64, SL], F32); zk = P.sbuf("zk", [64, SL], F32); zv = P.sbuf("zv", [64, SL], F32)
        vb = P.sbuf("vb", [64, SL], BF16)
        logw = P.sbuf("logw", [64, SL], F32); a_t = P.sbuf("a_t", [64, SL], F32); a_o = P.sbuf("a_o", [64, SL], F32)
        kap = P.sbuf("kap", [64, SL], F32); kd = P.sbuf("kd", [64, SL], F32); bb = P.sbuf("bb", [64, SL], F32)
        sqb = P.sbuf("sqb", [64, SL], BF16)
        bon = P.sbuf("bon", [64, SL], F32); g_t = P.sbuf("g_t", [64, SL], F32)
        YT = P.sbuf("YT", [64, SL], F32); yfh = P.sbuf("yfh", [64, SL], F32); orw = P.sbuf("orw", [64, SL], BF16)
        Lc = P.sbuf("Lc", [64, 128], F32); Lx = P.sbuf("Lx", [64, 128], F32)
        Ep = P.sbuf("Ep", [64, 128], F32); Em = P.sbuf("Em", [64, 128], F32); Ex = P.sbuf("Ex", [64, 128], F32)
        Ea = P.sbuf("Ea", [64, 128], F32); Exa = P.sbuf("Exa", [64, 128], F32)
        scol = P.sbuf("scol", [64, 4], F32)
        KR = P.sbuf("KR", [64, 256], BF16); Kt = P.sbuf("Kt", [64, 128], BF16); Bt = P.sbuf("Bt", [64, 128], BF16)
        F5 = P.sbuf("F5", [64, 5, 128], BF16)
        TM = P.sbuf("TM", [128, 5, 64], BF16)
        NBm = P.sbuf("NBm", [128, 256], BF16); NKm = P.sbuf("NKm", [128, 256], BF16); N1 = P.sbuf("N1", [128, 128], BF16)
        PP = [P.sbuf("PP", [128, 256], BF16) for _ in range(2)]
        X32 = P.sbuf("X32", [128, 128], F32); Xb = P.sbuf("Xb", [128, 128], BF16); Xn = P.sbuf("Xn", [128, 128], BF16)
        RhT = P.sbuf("RhT", [64, 128], BF16); AT = P.sbuf("AT", [64, 64], BF16)
        Sb = [P.sbuf("Sb%d" % h, [64, 64], BF16) for h in range(8)]

        def load_unit(u, d, t0, dst, dt_out=None, func=None):
            z = zp[u % 2]
            r0 = 64 * u
            P.dma("sp", z[:, 1:SL + 1], zsc[r0:r0 + 64, t0:t0 + SL], z, reads=[zsc], writes=[z])
            for (col, tt_, cond) in ((0, t0 - 1, t0 > 0), (SL + 1, t0 + SL, t0 + SL < T)):
                if cond:
                    P.dma("sp", z[:, col:col + 1], zsc[r0:r0 + 64, tt_:tt_ + 1], z, reads=[zsc], writes=[z], allow_slow_non_contiguous=True)
                    bnd = (t0 % SEG == 0) if col == 0 else ((t0 + SL) % SEG == 0)
                    if bnd:
                        P.ts("dve", z[:, col:col + 1], z[:, col:col + 1], lk[0:64, 0:1], ALU.mult, R=[z, lk], W=[z], tiny=True)
                else:
                    P.op("dve", lambda e, col=col: e.memset(z[:, col:col + 1], 0.0), [], [z], tiny=True)
            if d == 0:
                lo, mid, hi = z[:, 0:SL], z[:, 1:SL + 1], z[:, 2:SL + 2]
            else:
                lo, mid, hi = z[:, SL - 1::-1], z[:, SL:0:-1], z[:, SL + 1:1:-1]
            P.tt("pool", tmp1[:], lo, hi, ALU.add, R=[z], W=[tmp1])
            P.stt(tmp1[:], tmp1[:], 0.5, mid, ALU.mult, ALU.subtract, R=[tmp1, z], W=[tmp1])
            if func is None:
                P.stt(dst[:], tmp1[:], rmu[:, u:u + 1], mid, ALU.mult, ALU.add, R=[tmp1, z, rmu], W=[dst])
            else:
                P.stt(tmp2[:], tmp1[:], rmu[:, u:u + 1], mid, ALU.mult, ALU.add, R=[tmp1, z, rmu], W=[tmp2])
                P.act(dst[:], tmp2[:], func, R=[tmp2], W=[dst])

        for d in range(2):
            secs = list(range(T // SL)) if d == 0 else list(range(T // SL - 1, -1, -1))
            for sidx, sec in enumerate(secs):
                t0 = sec * SL
                load_unit(24 + d, d, t0, zlo[0], func=AF.Tanh)
                load_unit(26 + d, d, t0, zlo[1], func=AF.Copy)
                if d == 1:
                    load_unit(26, d, t0, zlo[2], func=AF.Copy)
                    load_unit(28, d, t0, zlo[3], func=AF.Sigmoid)
                    load_unit(29, d, t0, zlo[4], func=AF.Sigmoid)
                for h in range(8):
                    load_unit(h, d, t0, zr)
                    load_unit(8 + h, d, t0, zk)
                    load_unit(16 + h, d, t0, zv)
                    P.copy("pool", vb[:], zv[:], [zv], [vb])
                    hc = slice(h * 64, (h + 1) * 64)
                    col = lambda base, dd=None: rc[:, base + (h if dd is None else dd * 8 + h):base + (h if dd is None else dd * 8 + h) + 1]
                    ps = PS.f32()
                    P.mm(ps[0:64, 0:SL], w2b[:, d, hc], zlo[0][:], R=[w2b, zlo[0]], W=[ps])
                    P.act(logw[:], ps[0:64, 0:SL], AF.Sigmoid, bias=col(0, d), R=[ps, rc], W=[logw])
                    P.ts("pool", logw[:], logw[:], -DECAY_SCALE, ALU.mult, R=[logw], W=[logw])
                    ps = PS.f32()
                    P.mm(ps[0:64, 0:SL], a2b[:, d, hc], zlo[1][:], R=[a2b, zlo[1]], W=[ps])
                    P.act(a_t[:], ps[0:64, 0:SL], AF.Sigmoid, bias=col(16, d), R=[ps, rc], W=[a_t])
                    P.ts("dve", kap[:], zk[:], col(48), ALU.mult, R=[zk, rc], W=[kap])
                    P.act(sqb[:], kap[:], AF.Square, R=[kap], W=[sqb])
                    ps = PS.f32()
                    P.mm(ps[0:64, 0:SL], ones[0:64, 0:64], sqb[:], R=[ones, sqb], W=[ps])
                    P.ts("dve", tmp2[:], ps[0:64, 0:SL], 1e-24, ALU.max, R=[ps], W=[tmp2])
                    P.act(tmp2[:], tmp2[:], AF.Ln, R=[tmp2], W=[tmp2])
                    P.act(tmp2[:], tmp2[:], AF.Exp, scale=-0.5, R=[tmp2], W=[tmp2])
                    P.tt("dve", kap[:], kap[:], tmp2[:], ALU.mult, R=[kap, tmp2], W=[kap])
                    P.ts("dve", tmp2[:], a_t[:], col(32, d), ALU.mult, omka[:, d * 8 + h:d * 8 + h + 1], ALU.add, R=[a_t, rc, omka], W=[tmp2])
                    P.tt("dve", kd[:], tmp2[:], zk[:], ALU.mult, R=[tmp2, zk], W=[kd])
                    P.tt("pool", bb[:], kap[:], a_t[:], ALU.mult, R=[kap, a_t], W=[bb])
                    if d == 1:
                        ps = PS.f32()
                        P.mm(ps[0:64, 0:SL], a2b[:, 0, hc], zlo[2][:], R=[a2b, zlo[2]], W=[ps])
                        P.act(a_o[:], ps[0:64, 0:SL], AF.Sigmoid, bias=col(16, 0), R=[ps, rc], W=[a_o])
                        P.ts("dve", a_o[:], a_o[:], col(32, 0), ALU.mult, omka[:, h:h + 1], ALU.add, R=[a_o, rc, omka], W=[a_o])
                        P.tt("dve", a_o[:], a_o[:], tmp2[:], ALU.add, R=[a_o, tmp2], W=[a_o])
                        P.tt("dve", a_o[:], a_o[:], zk[:], ALU.mult, R=[a_o, zk], W=[a_o])
                        P.stt(sqb[:], a_o[:], col(72), zr[:], ALU.mult, ALU.mult, R=[a_o, rc, zr], W=[sqb])
                        ps = PS.f32()
                        P.mm(ps[0:64, 0:SL], ones[0:64, 0:64], sqb[:], R=[ones, sqb], W=[ps])
                        P.tt("dve", bon[:], ps[0:64, 0:SL], zv[:], ALU.mult, R=[ps, zv], W=[bon])
                        ps = PS.f32()
                        P.mm(ps[0:64, 0:SL], g2b[:, 0, hc], zlo[3][:], start=True, stop=False, R=[g2b, zlo[3]], W=[ps])
                        P.mm(ps[0:64, 0:SL], g2b[:, 1, hc], zlo[4][:], start=False, stop=True, R=[g2b, zlo[4]], W=[ps])
                        P.copy("act", g_t[:], ps[0:64, 0:SL], [ps], [g_t])
                    S = Sb[h]
                    for c in range(NCH):
                        cs = slice(c * 128, (c + 1) * 128)
                        gstart = (t0 + c * 128) if d == 0 else (t0 + SL - c * 128)
                        if sidx == 0 and c == 0:
                            P.op("dve", lambda e, S=S: e.memset(S[:], 0.0), [], [S])
                        elif gstart % SEG == 0:
                            P.ts("dve", S[:], S[:], lk[0:64, 0:1], ALU.mult, R=[S, lk], W=[S])
                        P.op("dve", lambda e, cs=cs: e.tensor_tensor_scan(out=Lc[:], data0=onesf[0:64, 0:128], data1=logw[:, cs], initial=0.0, op0=ALU.mult, op1=ALU.add), [onesf, logw], [Lc])
                        P.ts("dve", scol[:, 0:1], Lc[:, 63:64], -1.0, ALU.mult, R=[Lc], W=[scol])
                        P.copy("dve", scol[:, 1:2], Lc[:, 63:64], [Lc], [scol])
                        P.tt("pool", Lx[:], Lc[:], logw[:, cs], ALU.subtract, R=[Lc, logw], W=[Lx])
                        P.act(Ep[:], Lc[:], AF.Exp, bias=scol[:, 0:1], R=[Lc, scol], W=[Ep])
                        P.act(Em[:], Lc[:], AF.Exp, bias=scol[:, 1:2], scale=-1.0, R=[Lc, scol], W=[Em])
                        P.act(Ex[:], Lx[:], AF.Exp, bias=scol[:, 0:1], R=[Lx, scol], W=[Ex])
                        P.act(Ea[:], Lc[:], AF.Exp, R=[Lc], W=[Ea])
                        P.act(Exa[:], Lx[:], AF.Exp, R=[Lx], W=[Exa])
                        P.tt("dve", KR[:, 0:128], kap[:, cs], Ex[:], ALU.mult, R=[kap, Ex], W=[KR])
                        P.tt("pool", KR[:, 128:256], zr[:, cs], Ep[:], ALU.mult, R=[zr, Ep], W=[KR])
                        P.tt("dve", Kt[:], kd[:, cs], Em[:], ALU.mult, R=[kd, Em], W=[Kt])
                        P.tt("pool", Bt[:], bb[:, cs], Em[:], ALU.mult, R=[bb, Em], W=[Bt])
                        P.copy("pool", F5[:, 0, :], vb[:, cs], [vb], [F5])
                        P.ts("dve", F5[:, 1, :], Kt[:], Ep[:, 127:128], ALU.mult, R=[Kt, Ep], W=[F5])
                        P.ts("dve", F5[:, 2, :], Bt[:], Ep[:, 127:128], ALU.mult, R=[Bt, Ep], W=[F5])
                        P.tt("pool", F5[:, 3, :], kap[:, cs], Exa[:], ALU.mult, R=[kap, Exa], W=[F5])
                        P.tt("dve", F5[:, 4, :], zr[:, cs], Ea[:], ALU.mult, R=[zr, Ea], W=[F5])
                        pt = PS.bf()
                        for i in range(5):
                            P.tr(pt[:, i * 64:(i + 1) * 64], F5[:, i, :], ident[0:64, 0:64], [F5, ident], [pt])
                        P.copy("act", TM[:].rearrange("p a b -> p (a b)"), pt[:, 0:320], [pt], [TM])
                        Vt = TM[:, 0, :]; Kbt = TM[:, 1, :]; Bbt = TM[:, 2, :]; kabt = TM[:, 3, :]; Rabt = TM[:, 4, :]
                        pa = PS.f32()
                        P.mm(pa[:, 0:256], Bt[:], KR[:], R=[Bt, KR], W=[pa])
                        P.tt("dve", NBm[:], pa[:, 0:256], maskA[:], ALU.mult, R=[pa, maskA], W=[NBm])
                        pb = PS.f32()
                        P.mm(pb[:, 0:256], Kt[:], KR[:], R=[Kt, KR], W=[pb])
                        P.tt("dve", NKm[:], pb[:, 0:256], maskA[:], ALU.mult, R=[pb, maskA], W=[NKm])
                        pc = PS.f32()
                        P.mm(pc[:, 0:128], KR[:, 0:128], Bt[:], R=[Bt, KR], W=[pc])
                        P.tt("dve", N1[:], pc[:, 0:128], maskL[:], ALU.mult, R=[pc, maskL], W=[N1])
                        px = PS.f32()
                        P.mm(px[:, 0:64], NKm[:, 0:128], Vt, R=[NKm, TM], W=[px])
                        P.copy("pool", X32[:, 0:64], kabt, [TM], [X32])
                        P.copy("act", X32[:, 64:128], px[:, 0:64], [px], [X32])
                        P.copy("pool", Xb[:], X32[:], [X32], [Xb])
                        py = PS.f32()
                        P.mm(py[:, 0:128], NBm[:, 0:128], Xb[:], R=[NBm, Xb], W=[py])
                        P.tt("dve", X32[:], X32[:], py[:, 0:128], ALU.subtract, R=[X32, py], W=[X32])
                        P.copy("pool", Xb[:], X32[:], [X32], [Xb])
                        Pm, PTm, Pb_, PTb = N1[:], NBm[:, 0:128], N1, NBm
                        for lvl in range(6):
                            pp = PP[lvl % 2]
                            psq = PS.f32()
                            P.mm(psq[:, 0:128], PTm, Pm, R=[Pb_, PTb], W=[psq])
                            P.mm(psq[:, 128:256], Pm, PTm, R=[Pb_, PTb], W=[psq])
                            P.copy("act", pp[:], psq[:, 0:256], [psq], [pp])
                            Pm, PTm, Pb_, PTb = pp[:, 0:128], pp[:, 128:256], pp, pp
                            py = PS.f32()
                            P.mm(py[:, 0:128], PTm, Xb[:], R=[pp, Xb], W=[py])
                            P.tt("dve", X32[:], X32[:], py[:, 0:128], ALU.add, R=[X32, py], W=[X32])
                            if lvl < 5:
                                P.copy("pool", Xb[:], X32[:], [X32], [Xb])
                            else:
                                P.ts("dve", Xn[:], X32[:], -1.0, ALU.mult, R=[X32], W=[Xn])
                        nkab = Xn[:, 0:64]; nU = Xn[:, 64:128]
                        pr = PS.f32()
                        P.mm(pr[0:64, 0:128], Rabt, ident[:], start=True, stop=False, R=[TM, ident], W=[pr])
                        P.mm(pr[0:64, 0:128], nkab, NBm[:, 128:256], start=False, stop=True, R=[Xn, NBm], W=[pr])
                        P.copy("act", RhT[:], pr[0:64, 0:128], [pr], [RhT])
                        pat = PS.f32()
                        P.mm(pat[0:64, 0:64], nkab, Bbt, R=[Xn, TM], W=[pat])
                        P.stt(AT[:], identf[0:64, 0:64], Ea[:, 127:128], pat[0:64, 0:64], ALU.mult, ALU.add, R=[identf, Ea, pat], W=[AT])
                        pyl = PS.f32()
                        P.mm(pyl[0:64, 0:128], Vt, NKm[:, 128:256], start=True, stop=False, R=[TM, NKm], W=[pyl])
                        P.mm(pyl[0:64, 0:128], nU, NBm[:, 128:256], start=False, stop=False, R=[Xn, NBm], W=[pyl])
                        P.mm(pyl[0:64, 0:128], S[:], RhT[:], start=False, stop=True, R=[S, RhT], W=[pyl])
                        P.copy("act", YT[:, cs], pyl[0:64, 0:128], [pyl], [YT])
                        pg = PS.f32()
                        P.mm(pg[0:64, 0:64], Kbt, Vt, start=True, stop=False, R=[TM], W=[pg])
                        P.mm(pg[0:64, 0:64], Bbt, nU, start=False, stop=False, R=[TM, Xn], W=[pg])
                        P.mm(pg[0:64, 0:64], AT[:], S[:], start=False, stop=True, R=[AT, S], W=[pg])
                        P.copy("dve", S[:], pg[0:64, 0:64], [pg], [S])
                    rows = slice(h * 64, (h + 1) * 64)
                    if d == 0:
                        P.dma("pool", yfs[rows, t0:t0 + SL], YT[:], YT, reads=[YT])
                    else:
                        P.dma("sp", yfh[:], yfs[rows, t0:t0 + SL], yfh, reads=[yfs], writes=[yfh])
                        P.tt("dve", YT[:], YT[:], yfh[:, ::-1], ALU.add, R=[YT, yfh], W=[YT])
                        P.copy("pool", sqb[:], YT[:], [YT], [sqb])
                        ps = PS.f32()
                        P.mm(ps[0:64, 0:SL], ones64m[:], sqb[:], R=[ones64m, sqb], W=[ps])
                        P.tt("dve", YT[:], YT[:], ps[0:64, 0:SL], ALU.subtract, R=[YT, ps], W=[YT])
                        P.act(sqb[:], YT[:], AF.Square, R=[YT], W=[sqb])
                        ps = PS.f32()
                        P.mm(ps[0:64, 0:SL], ones64m[:], sqb[:], R=[ones64m, sqb], W=[ps])
                        P.ts("dve", tmp2[:], ps[0:64, 0:SL], GN_EPS, ALU.add, R=[ps], W=[tmp2])
                        P.act(tmp2[:], tmp2[:], AF.Ln, R=[tmp2], W=[tmp2])
                        P.act(tmp2[:], tmp2[:], AF.Exp, scale=-0.5, R=[tmp2], W=[tmp2])
                        P.tt("dve", YT[:], YT[:], tmp2[:], ALU.mult, R=[YT, tmp2], W=[YT])
                        P.ts("dve", YT[:], YT[:], col(56), ALU.mult, col(64), ALU.add, R=[YT, rc], W=[YT])
                        P.tt("pool", YT[:], YT[:], bon[:], ALU.add, R=[YT, bon], W=[YT])
                        P.tt("dve", orw[:, ::-1], YT[:], g_t[:], ALU.mult, R=[YT, g_t], W=[orw])
                        P.dma("pool", mixT[512 + h * 64:512 + (h + 1) * 64, t0:t0 + SL], orw[:], orw, reads=[orw])
            if d == 0:
                P.barrier()
        P.pop()

        P.push()
        SA = min(512, SEG)
        NQ = SA // 128
        Qs = P.sbuf("Qs", [128, 8, SA], BF16); Ks = P.sbuf("Ks", [128, 2, SA + 256], BF16)
        Vs = P.sbuf("Vs", [128, NQ + 2, 256], BF16); OT = P.sbuf("OT", [128, 8, SA], BF16)
        Sbt = [P.sbuf("Sbt", [128, 384], F32) for _ in range(2)]
        PT = [P.sbuf("PT", [128, 3, 4, 128], BF16) for _ in range(2)]
        den = P.sbuf("den", [128, 512], F32)
        esk = P.sbuf("esk", [128, 8], F32)
        P.dma("sp", esk[:], sink_in[l, :, :], esk, writes=[esk])
        P.act(esk[:], esk[:], AF.Exp, R=[esk], W=[esk])
        qv = qT.t.rearrange("(h p) t -> p h t", p=128)
        kv = kT.t.rearrange("(g p) t -> p g t", p=128)
        vv = vtm.t.rearrange("(b p) c -> p b c", p=128)
        mv = mixT.t.rearrange("(c p) t -> p c t", p=128)
        pi = 0
        for sec in range(T // SA):
            t0 = sec * SA
            P.dma("sp", Qs[:], qv[:, :, t0:t0 + SA], Qs, reads=[qT], writes=[Qs])
            lo = max(0, t0 - 128); hi = min(T, t0 + SA + 128)
            P.dma("sp", Ks[:, :, lo - (t0 - 128):hi - (t0 - 128)], kv[:, :, lo:hi], Ks, reads=[kT], writes=[Ks])
            P.dma("sp", Vs[:, (lo - (t0 - 128)) // 128:(hi - (t0 - 128)) // 128, :], vv[:, lo // 128:hi // 128, :], Vs, reads=[vtm], writes=[Vs])
            for i in range(NQ):
                n = sec * NQ + i
                js = [j for j in (-1, 0, 1) if 0 <= n + j < NB]
                jlo = (js[0] + 1) * 128; jhi = (js[-1] + 2) * 128
                for g in range(2):
                    pt = PT[pi % 2]; pi += 1
                    for h in range(4):
                        hh = 4 * g + h
                        ps = PS.f32()
                        for j in js:
                            P.mm(ps[:, (j + 1) * 128:(j + 2) * 128], Ks[:, g, (i + j + 1) * 128:(i + j + 2) * 128], Qs[:, hh, i * 128:(i + 1) * 128], R=[Ks, Qs], W=[ps])
                        sb_ = Sbt[h % 2]
                        P.tt("dve", sb_[:, jlo:jhi], ps[:, jlo:jhi], biasT[:, hh, jlo:jhi], ALU.add, R=[ps, biasT], W=[sb_])
                        for j in js:
                            cross = ((n + j) // NBSEG) != (n // NBSEG)
                            if cross:
                                P.act(pt[:, j + 1, h, :], sb_[:, (j + 1) * 128:(j + 2) * 128], AF.Exp, bias=NEGM, R=[sb_, lk], W=[pt])
                            else:
                                P.act(pt[:, j + 1, h, :], sb_[:, (j + 1) * 128:(j + 2) * 128], AF.Exp, R=[sb_], W=[pt])
                    pd = PS.f32()
                    for j in js:
                        P.mm(pd[:, :], ones[:], pt[:, j + 1, :, :].rearrange("p a b -> p (a b)"), start=(j == js[0]), stop=(j == js[-1]), R=[ones, pt], W=[pd])
                    po = PS.f32()
                    for h in range(4):
                        for j in js:
                            P.mm(po[:, h * 128:(h + 1) * 128], Vs[:, i + j + 1, g * 128:(g + 1) * 128], pt[:, j + 1, h, :], start=(j == js[0]), stop=(j == js[-1]), R=[Vs, pt], W=[po])
                    for h in range(4):
                        P.ts("dve", den[:, h * 128:(h + 1) * 128], pd[:, h * 128:(h + 1) * 128], esk[:, 4 * g + h:4 * g + h + 1], ALU.add, R=[pd, esk], W=[den])
                    P.op("dve", lambda e: e.reciprocal(out=den[:], in_=den[:]), [den], [den])
                    for h in range(4):
                        P.tt("dve", OT[:, 4 * g + h, i * 128:(i + 1) * 128], po[:, h * 128:(h + 1) * 128], den[:, h * 128:(h + 1) * 128], ALU.mult, R=[po, den], W=[OT])
            P.dma("pool", mv[:, 8:16, t0:t0 + SA], OT[:], OT, reads=[OT])
        P.pop()

        P.push()
        ME = 256
        mixs = P.sbuf("mixs", [128, 16, ME], BF16)
        wbe = [P.sbuf("wbe", [128, 16, 512], BF16) for _ in range(3)]
        osb = P.sbuf("osb", [128, 2, D], F32); fsb = P.sbuf("fsb", [128, 2, D], F32)
        xin = [P.sbuf("xin", [128, D], F32) for _ in range(2)]
        xn2 = [P.sbuf("xn2", [128, D], BF16) for _ in range(2)]
        h2T = P.sbuf("h2T", [128, 16, ME], BF16); hid = P.sbuf("hid", [128, 64, ME], BF16)
        gB1 = P.sbuf("gB1", [128, D], F32); gB2 = P.sbuf("gB2", [128, D], F32); gc2 = P.sbuf("gc2", [128, 16], F32)
        sqt = P.sbuf("sqt", [128, ME], F32)
        P.dma("sp", gB1[:], gpost1[l, :, :], gB1, writes=[gB1])
        P.dma("sp", gB2[:], gpost2[l, :, :], gB2, writes=[gB2])
        P.dma("sp", gc2[:], gpre2[l, :, :], gc2, writes=[gc2])
        wo = wb_out[l].t.rearrange("(k p) c -> p k c", p=128)
        wu = wb_up[l].t.rearrange("(k p) c -> p k c", p=128)
        wd = wb_dn[l].t.rearrange("(f p) c -> p f c", p=128)
        mk = mixT.t.rearrange("(k p) t -> p k t", p=128)
        wi = 0
        if dbg and l == nl - 1:
            dbg_o = P.dram("dbg_o", [T, D], F32); dbg_x1 = P.dram("dbg_x1", [T, D], F32); dbg_f = P.dram("dbg_f", [T, D], F32)
            dbg_hid = P.dram("dbg_hid", [DFF, T], BF16)
        for m in range(T // ME):
            t0 = m * ME
            P.dma("sp", mixs[:], mk[:, :, t0:t0 + ME], mixs, reads=[mixT], writes=[mixs])
            for n4 in range(4):
                w = wbe[wi % 3]; wi += 1
                P.dma("sp", w[:], wo[:, :, n4 * 512:(n4 + 1) * 512], w, writes=[w])
                for sub in range(2):
                    ps = PS.f32()
                    for k in range(16):
                        P.mm(ps[:, :], mixs[:, k, sub * 128:(sub + 1) * 128], w[:, k, :], start=(k == 0), stop=(k == 15), R=[mixs, w], W=[ps])
                    P.copy("act" if sub else "dve", osb[:, sub, n4 * 512:(n4 + 1) * 512], ps[:, :], [ps], [osb])
            for sub in range(2):
                xi = xin[sub]; xb_ = xn2[sub]
                if dbg and l == nl - 1:
                    P.dma("pool", dbg_o[t0 + sub * 128:t0 + (sub + 1) * 128, :], osb[:, sub, :], osb, reads=[osb])
                P.dma("sp", xi[:], xcur[t0 + sub * 128:t0 + (sub + 1) * 128, :], xi, writes=[xi])
                P.act(xb_[:], osb[:, sub, :], AF.Square, R=[osb], W=[xb_])
                P.op("dve", lambda e, junk=xb_: e.reduce_sum(out=SM[:, 4:5], in_=junk[:], axis=mybir.AxisListType.X), [xb_], [SM])
                rstd_from_ssq(SM[:, 4:5], SM[:, 5:6], SM[:, 6:7], D)
                P.stt(osb[:, sub, :], osb[:, sub, :], SM[:, 6:7], gB1[:], ALU.mult, ALU.mult, R=[osb, SM, gB1], W=[osb])
                P.tt("pool", osb[:, sub, :], osb[:, sub, :], xi[:], ALU.add, R=[osb, xi], W=[osb])
                if dbg and l == nl - 1:
                    P.dma("pool", dbg_x1[t0 + sub * 128:t0 + (sub + 1) * 128, :], osb[:, sub, :], osb, reads=[osb])
                P.act(xb_[:], osb[:, sub, :], AF.Square, R=[osb], W=[xb_])
                P.op("dve", lambda e, junk=xb_: e.reduce_sum(out=SM[:, 7:8], in_=junk[:], axis=mybir.AxisListType.X), [xb_], [SM])
                rstd_from_ssq(SM[:, 7:8], SM[:, 8:9], SM[:, 9:10], D)
                P.ts("dve", xb_[:], osb[:, sub, :], SM[:, 9:10], ALU.mult, R=[osb, SM], W=[xb_])
                for kq in range(4):
                    pt = PS.bf()
                    for kk in range(4):
                        k = kq * 4 + kk
                        P.tr(pt[:, kk * 128:(kk + 1) * 128], xb_[:, k * 128:(k + 1) * 128], ident[:], [xb_, ident], [pt])
                    for kk in range(4):
                        k = kq * 4 + kk
                        if kk % 2:
                            P.act(h2T[:, k, sub * 128:(sub + 1) * 128], pt[:, kk * 128:(kk + 1) * 128], AF.Copy, scale=gc2[:, k:k + 1], R=[pt, gc2], W=[h2T])
                        else:
                            P.ts("dve", h2T[:, k, sub * 128:(sub + 1) * 128], pt[:, kk * 128:(kk + 1) * 128], gc2[:, k:k + 1], ALU.mult, R=[pt, gc2], W=[h2T])
            for fg in range(16):
                w = wbe[wi % 3]; wi += 1
                P.dma("sp", w[:], wu[:, :, fg * 512:(fg + 1) * 512], w, writes=[w])
                for fcc in range(4):
                    fc = fg * 4 + fcc
                    ps = PS.f32()
                    for k in range(16):
                        P.mm(ps[:, 0:ME], w[:, k, fcc * 128:(fcc + 1) * 128], h2T[:, k, :], start=(k == 0), stop=(k == 15), R=[w, h2T], W=[ps])
                    P.act(sqt[:], ps[:, 0:ME], AF.Square, R=[ps], W=[sqt])
                    P.stt(hid[:, fc, :], ps[:, 0:ME], 0.0, sqt[:], ALU.is_gt, ALU.mult, R=[ps, sqt], W=[hid])
            for n4 in range(4):
                pacc = [PS.f32(), PS.f32()]
                for kq in range(4):
                    w = wbe[wi % 3]; wi += 1
                    P.dma("sp", w[:], wd[:, kq * 16:(kq + 1) * 16, n4 * 512:(n4 + 1) * 512], w, writes=[w])
                    for sub in range(2):
                        for kk in range(16):
                            P.mm(pacc[sub][:, :], hid[:, kq * 16 + kk, sub * 128:(sub + 1) * 128], w[:, kk, :], start=(kq == 0 and kk == 0), stop=(kq == 3 and kk == 15), R=[hid, w], W=[pacc[sub]])
                for sub in range(2):
                    P.copy("act" if sub else "dve", fsb[:, sub, n4 * 512:(n4 + 1) * 512], pacc[sub][:, :], [pacc[sub]], [fsb])
            if dbg and l == nl - 1:
                P.dma("pool", dbg_hid.t.rearrange("(f p) t -> p f t", p=128)[:, :, t0:t0 + ME], hid[:], hid, reads=[hid])
            for sub in range(2):
                xb_ = xn2[sub]
                if dbg and l == nl - 1:
                    P.dma("pool", dbg_f[t0 + sub * 128:t0 + (sub + 1) * 128, :], fsb[:, sub, :], fsb, reads=[fsb])
                P.act(xb_[:], fsb[:, sub, :], AF.Square, R=[fsb], W=[xb_])
                P.op("dve", lambda e, junk=xb_: e.reduce_sum(out=SM[:, 10:11], in_=junk[:], axis=mybir.AxisListType.X), [xb_], [SM])
                rstd_from_ssq(SM[:, 10:11], SM[:, 11:12], SM[:, 12:13], D)
                P.stt(fsb[:, sub, :], fsb[:, sub, :], SM[:, 12:13], gB2[:], ALU.mult, ALU.mult, R=[fsb, SM, gB2], W=[fsb])
                P.tt("pool", fsb[:, sub, :], fsb[:, sub, :], osb[:, sub, :], ALU.add, R=[fsb, osb], W=[fsb])
                P.dma("pool", xnext[t0 + sub * 128:t0 + (sub + 1) * 128, :], fsb[:, sub, :], fsb, reads=[fsb])
        P.pop()
        xcur = xnext
    P.barrier()
    es.close()
    return nc


def _t5_bucket_np(rel):
    half = 16; max_exact = 8
    ret = np.where(rel > 0, half, 0)
    n = np.abs(rel)
    nf = np.maximum(n, 1).astype(np.float32)
    large = max_exact + (np.log(nf / max_exact) / math.log(128 / max_exact) * (half - max_exact)).astype(np.int32)
    large = np.minimum(large, half - 1)
    return ret + np.where(n < max_exact, n, large)


def host_layout(inp):
    f = lambda a: np.ascontiguousarray(a, dtype=np.float32)
    L = NL
    m = {}
    m["w_in"] = f(inp["w_in"]); m["w_out"] = f(inp["w_out"]); m["w_up"] = f(inp["w_up"]); m["w_down"] = f(inp["w_down"])
    m["gpre1"] = f(inp["norm_mix_pre"].reshape(L, 16, 128).transpose(0, 2, 1))
    m["gpre2"] = f(inp["norm_ffn_pre"].reshape(L, 16, 128).transpose(0, 2, 1))
    m["gpost1"] = f(np.broadcast_to(inp["norm_mix_post"][:, None, :], (L, 128, D)))
    m["gpost2"] = f(np.broadcast_to(inp["norm_ffn_post"][:, None, :], (L, 128, D)))
    lc = np.zeros((L, 128, 48), np.float32)
    lc[:, :, 0:16] = inp["conv_w"].reshape(L, 4, 4, 128).transpose(0, 3, 2, 1).reshape(L, 128, 16)
    lc[:, :, 16:20] = inp["conv_b"].reshape(L, 4, 128).transpose(0, 2, 1)
    lc[:, :, 20:28] = inp["lru_ba"].reshape(L, 2, 4, 128).transpose(0, 3, 1, 2).reshape(L, 128, 8)
    lc[:, :, 28:36] = inp["lru_bx"].reshape(L, 2, 4, 128).transpose(0, 3, 1, 2).reshape(L, 128, 8)
    lc[:, :, 36:44] = inp["lru_lambda"].reshape(L, 2, 4, 128).transpose(0, 3, 1, 2).reshape(L, 128, 8)
    m["lru_cols"] = lc
    bd = np.zeros((L, 2, 2, 4, 128, 128), np.float32)
    for gi, nm in enumerate(("lru_wa", "lru_wx")):
        w = inp[nm]
        for ct in range(4):
            bd[:, gi, :, ct, 0:64, 0:64] = w[:, :, 2 * ct]
            bd[:, gi, :, ct, 64:128, 64:128] = w[:, :, 2 * ct + 1]
    m["lru_bd"] = bd.reshape(L, 16, 128, 128)
    m["rk_mu"] = f(inp["rwkv_mu"].reshape(L, 30, 64).transpose(0, 2, 1))
    rc = np.zeros((L, 64, 80), np.float32)
    dh = lambda a: a.reshape(L, 2, 8, 64).transpose(0, 3, 1, 2).reshape(L, 64, 16)
    hh = lambda a: a.reshape(L, 8, 64).transpose(0, 2, 1)
    rc[:, :, 0:16] = dh(inp["rwkv_w0"]); rc[:, :, 16:32] = dh(inp["rwkv_a0"]); rc[:, :, 32:48] = dh(inp["rwkv_k_a"])
    rc[:, :, 48:56] = hh(inp["rwkv_k_k"]); rc[:, :, 56:64] = hh(inp["rwkv_gn_w"]); rc[:, :, 64:72] = hh(inp["rwkv_gn_b"])
    rc[:, :, 72:80] = hh(inp["rwkv_r_k"])
    m["rk_cols"] = rc
    m["rk_w2"] = f(inp["rwkv_w2"]); m["rk_a2"] = f(inp["rwkv_a2"])
    m["rk_g2"] = f(inp["rwkv_g2"].reshape(L, 2, 64, 512).transpose(0, 2, 1, 3))
    m["sink"] = f(np.broadcast_to(inp["attn_sink"][:, None, :], (L, 128, 8)))
    m["relb"] = f(inp["rel_bias"])
    rel = np.arange(768) - 384
    oh = np.zeros((33, 768), np.float32)
    bk = _t5_bucket_np(rel)
    valid = np.abs(rel) <= 128
    oh[bk[valid], np.nonzero(valid)[0]] = 1.0
    oh[32, ~valid] = 1.0
    m["oh"] = oh
    return m


def _lk(link):
    a = np.zeros((128, 4), np.float32)
    a[:, 0] = link; a[:, 1] = 1.0 - link; a[:, 2] = (link - 1.0) * 30000.0
    return a


_NC_CACHE = {}


def kernel(**inputs):
    inp = {k: np.asarray(v) for k, v in inputs.items()}
    xp = inp["x_prompt"].astype(np.float32); xs = inp["x_sample"].astype(np.float32)
    T = 8192
    m = host_layout(inp)
    in_maps = []
    for c in range(4):
        d = dict(m); d["x"] = np.ascontiguousarray(xs[c]); d["lk"] = _lk(1.0); in_maps.append(d)
    for c in range(4):
        d = dict(m)
        d["x"] = np.ascontiguousarray(np.concatenate([xp[2 * c], xp[2 * c + 1], xp[2 * c], xp[2 * c + 1]], axis=0))
        d["lk"] = _lk(0.0); in_maps.append(d)
    nc = build(T)
    res = run_bass_kernel_spmd(nc, in_maps, core_ids=list(range(8)))
    ys = np.stack([res.results[c]["y"] for c in range(4)], axis=0).astype(np.float32)
    yp = np.empty_like(xp)
    for c in range(4):
        y = res.results[4 + c]["y"]
        yp[2 * c] = y[0:2048]; yp[2 * c + 1] = y[2048:4096]
    return (yp, ys)
```

```python
import math
import numpy as np
from contextlib import ExitStack
import concourse.bass as bass
import concourse.mybir as mybir
from concourse.bass_utils import run_bass_kernel_spmd

F32 = mybir.dt.float32
BF16 = mybir.dt.bfloat16
AF = mybir.ActivationFunctionType
ALU = mybir.AluOpType

D = 2048; DIN = 4480; DFF = 8192; NL = 2
EPS = 1e-6; GN_EPS = 64e-5; DECAY_SCALE = 0.606531
GC1 = math.sqrt(2.0 / math.pi); GC2 = GC1 * 0.044715


class Buf:
    __slots__ = ("t", "name", "lw", "rd", "sem", "cnt", "strict", "tiny")

    def __init__(self, t, name, strict=False):
        self.t = t; self.name = name; self.lw = None; self.rd = {}; self.sem = None; self.cnt = 0
        self.strict = strict; self.tiny = False

    def __getitem__(self, idx):
        return self.t[idx]


ENGS = {"pe": "tensor", "act": "scalar", "dve": "vector", "pool": "gpsimd", "sp": "sync"}


class Prog:
    def __init__(self, nc, es):
        self.nc = nc; self.es = es
        self.eng = {k: getattr(nc, v) for k, v in ENGS.items()}
        self.sem = {k: es.enter_context(nc.semaphore("sem_" + k)) for k in ENGS}
        self.cnt = {k: 0 for k in ENGS}
        self.waited = {k: {} for k in ENGS}
        self.dh = []; self.dc = []; self.free_slots = []; self.scope_slots = [[]]
        self.scopes = [es]
        self.uid = 0
        self._rec = None

    def push(self):
        s = ExitStack(); self.scopes.append(s); self.scope_slots.append([]); return s

    def pop(self):
        self.barrier()
        self.free_slots.extend(self.scope_slots.pop())
        self.scopes.pop().close()

    def sbuf(self, name, shape, dt):
        self.uid += 1
        t = self.scopes[-1].enter_context(self.nc.sbuf_tensor("%s_%d" % (name, self.uid), list(shape), dt))
        n = 1
        for d_ in shape[1:]:
            n *= d_
        return Buf(t, name, strict=(n <= 64))

    def psum(self, name, shape, dt):
        self.uid += 1
        t = self.scopes[-1].enter_context(self.nc.psum_tensor("%s_%d" % (name, self.uid), list(shape), dt))
        return Buf(t, name)

    def dram(self, name, shape, dt, kind="Internal"):
        if kind == "Internal" and getattr(self, "dbg", False) and not name.startswith("wb"):
            kind = "ExternalOutput"
        t = self.nc.dram_tensor(name, list(shape), dt, kind=kind)
        return Buf(t.ap(), name)

    def _semof(self, key):
        if key[0] == "e":
            return self.sem[key[1]]
        return self.dh[key[1]]

    def _deps(self, eng, reads, writes):
        deps = {}
        me = ("e", eng)
        selfv = 0
        for b in reads:
            if b.lw is not None:
                k, v = b.lw
                if deps.get(k, 0) < v: deps[k] = v
                if (b.strict or b.tiny) and k == me and v > selfv: selfv = v
        for b in writes:
            if b.lw is not None:
                k, v = b.lw
                if deps.get(k, 0) < v: deps[k] = v
                if (b.strict or b.tiny) and k == me and v > selfv: selfv = v
            for k, v in b.rd.items():
                if deps.get(k, 0) < v: deps[k] = v
                if (b.strict or b.tiny) and k == me and v > selfv: selfv = v
        w = self.waited[eng]
        e = self.eng[eng]
        for k, v in deps.items():
            if k == me:
                if selfv == 0 or w.get(k, 0) >= selfv: continue
                w[k] = selfv
                e.wait_ge(self.sem[eng], selfv)
                continue
            if w.get(k, 0) >= v: continue
            w[k] = v
            e.wait_ge(self._semof(k), v)

    def _mark(self, tok, reads, writes):
        for b in writes:
            b.lw = tok; b.rd = {}
        for b in reads:
            if b.lw is not tok:
                b.rd[tok[0]] = tok[1]

    def rec_begin(self):
        self._rec = []

    def rec_end(self):
        r = self._rec; self._rec = None
        return r

    def replay(self, it):
        if it[0] == "op":
            self.op(*it[1:])
        else:
            self.dma(*it[1:-1], **it[-1])

    def op(self, eng, fn, reads=(), writes=(), tiny=False):
        if self._rec is not None:
            self._rec.append(("op", eng, fn, tuple(reads), tuple(writes), tiny)); return
        self._deps(eng, reads, writes)
        self.cnt[eng] += 1
        tok = (("e", eng), self.cnt[eng])
        fn(self.eng[eng]).then_inc(self.sem[eng], 1)
        self._mark(tok, reads, writes)
        for b in writes:
            b.tiny = tiny

    def dma(self, q, out_ap, in_ap, sb, reads=(), writes=(), **kw):
        if self._rec is not None:
            self._rec.append(("dma", q, out_ap, in_ap, sb, tuple(reads), tuple(writes), kw)); return
        self._deps(q, reads, writes)
        if sb.sem is None:
            if self.free_slots:
                sb.sem = self.free_slots.pop()
            else:
                self.dh.append(self.es.enter_context(self.nc.semaphore("d%d" % len(self.dh))))
                self.dc.append(0)
                sb.sem = len(self.dh) - 1
            self.scope_slots[-1].append(sb.sem)
        i = sb.sem
        self.dc[i] += 16
        tok = (("d", i), self.dc[i])
        self.eng[q].dma_start(out=out_ap, in_=in_ap, **kw).then_inc(self.dh[i], 16)
        self._mark(tok, reads, writes)

    def barrier(self):
        for eng in ENGS:
            w = self.waited[eng]; e = self.eng[eng]
            for o in ENGS:
                if o == eng: continue
                k = ("e", o); v = self.cnt[o]
                if v > w.get(k, 0):
                    w[k] = v; e.wait_ge(self.sem[o], v)
            for i in range(len(self.dh)):
                k = ("d", i)
                if self.dc[i] > w.get(k, 0):
                    w[k] = self.dc[i]; e.wait_ge(self.dh[i], self.dc[i])

    def ts(self, eng, out, in0, s1, op0, s2=None, op1=None, R=(), W=(), tiny=False):
        if op1 is None:
            self.op(eng, lambda e: e.tensor_scalar(out=out, in0=in0, scalar1=s1, scalar2=None, op0=op0), R, W, tiny=tiny)
        else:
            self.op(eng, lambda e: e.tensor_scalar(out=out, in0=in0, scalar1=s1, scalar2=s2, op0=op0, op1=op1), R, W, tiny=tiny)

    def tt(self, eng, out, in0, in1, op, R=(), W=()):
        self.op(eng, lambda e: e.tensor_tensor(out=out, in0=in0, in1=in1, op=op), R, W)

    def stt(self, out, in0, s, in1, op0, op1, R=(), W=()):
        self.op("dve", lambda e: e.scalar_tensor_tensor(out=out, in0=in0, scalar=s, in1=in1, op0=op0, op1=op1), R, W)

    def act(self, out, in_, func, bias=None, scale=None, accum=None, R=(), W=()):
        kw = {}
        if bias is not None: kw["bias"] = bias
        if scale is not None: kw["scale"] = scale
        if accum is not None: kw["accum_out"] = accum
        self.op("act", lambda e: e.activation(out=out, in_=in_, func=func, **kw), R, W)

    def copy(self, eng, out, in_, R=(), W=()):
        if eng == "act":
            self.act(out, in_, AF.Copy, R=R, W=W)
        else:
            self.op(eng, lambda e: e.tensor_copy(out=out, in_=in_), R, W)

    def mm(self, out, lhsT, rhs, start=True, stop=True, R=(), W=()):
        self.op("pe", lambda e: e.matmul(out, lhsT=lhsT, rhs=rhs, start=start, stop=stop), R, W)

    def tr(self, out, in_, ident, R=(), W=()):
        self.op("pe", lambda e: e.transpose(out=out, in_=in_, identity=ident), R, W)


class PsumPool:
    def __init__(self, P):
        self.f = [P.psum("psf%d" % i, [128, 512], F32) for i in range(5)]
        self.b = [P.psum("psb%d" % i, [128, 1024], BF16) for i in range(3)]
        self.i = 0; self.j = 0

    def f32(self):
        self.i += 1
        return self.f[self.i % 5]

    def bf(self):
        self.j += 1
        return self.b[self.j % 3]


def build(T, dbg=False, nl=NL):
    SEG = T // 4
    NB = T // 128
    NBSEG = SEG // 128
    nc = bass.Bass("TRN2", target_bir_lowering=False)
    es = ExitStack()
    P = Prog(nc, es)
    P.dbg = dbg

    def din(name, shape, dt=F32):
        return P.dram(name, shape, dt, kind="ExternalInput")

    x_in = din("x", [T, D])
    lk_in = din("lk", [128, 4])
    w_in = din("w_in", [NL, D, DIN]); w_out = din("w_out", [NL, D, D])
    w_up = din("w_up", [NL, D, DFF]); w_dn = din("w_down", [NL, DFF, D])
    gpre1 = din("gpre1", [NL, 128, 16]); gpre2 = din("gpre2", [NL, 128, 16])
    gpost1 = din("gpost1", [NL, 128, D]); gpost2 = din("gpost2", [NL, 128, D])
    lru_cols = din("lru_cols", [NL, 128, 48])
    lru_bd = din("lru_bd", [NL, 16, 128, 128])
    rk_mu = din("rk_mu", [NL, 64, 30])
    rk_cols = din("rk_cols", [NL, 64, 80])
    rk_w2 = din("rk_w2", [NL, 2, 64, 512]); rk_a2 = din("rk_a2", [NL, 2, 64, 512])
    rk_g2 = din("rk_g2", [NL, 64, 2, 512])
    sink_in = din("sink", [NL, 128, 8])
    relb = din("relb", [32, 8])
    oh_in = din("oh", [33, 768])
    y_out = P.dram("y", [T, D], F32, kind="ExternalOutput")

    wb_in = [P.dram("wbin%d" % l, [D, DIN], BF16) for l in range(NL)]
    wb_out = [P.dram("wbout%d" % l, [D, D], BF16) for l in range(NL)]
    wb_up = [P.dram("wbup%d" % l, [D, DFF], BF16) for l in range(NL)]
    wb_dn = [P.dram("wbdn%d" % l, [DFF, D], BF16) for l in range(NL)]
    xs1 = P.dram("xs1", [T, D], F32)
    lrux = P.dram("lrux", [512, T], F32); lrug = P.dram("lrug", [512, T], BF16)
    zsc = P.dram("zsc", [1920, T], F32)
    qT = P.dram("qT", [1024, T], BF16); kT = P.dram("kT", [256, T], BF16); vtm = P.dram("vtm", [T, 256], BF16)
    mixT = P.dram("mixT", [D, T], BF16)
    hfs = P.dram("hfs", [512, T], F32); yfs = P.dram("yfs", [512, T], F32)
    btab = P.dram("btab", [8, 768], F32)

    PS = PsumPool(P)
    identf = P.sbuf("identf", [128, 128], F32)
    ident = P.sbuf("ident", [128, 128], BF16)
    ones = P.sbuf("ones", [128, 128], BF16)
    ones64m = P.sbuf("ones64m", [64, 64], BF16)
    maskA = P.sbuf("maskA", [128, 256], F32)
    maskL = P.sbuf("maskL", [128, 128], F32)
    onesf = P.sbuf("onesf", [128, 128], F32)
    lk = P.sbuf("lk", [128, 4], F32)
    biasT = P.sbuf("biasT", [128, 8, 384], F32)
    CONSTS = [identf, ident, ones, ones64m, maskA, maskL, onesf, lk, biasT]

    P.op("pool", lambda e: e.memset(identf[:], 1.0), [], [identf])
    P.op("pool", lambda e: e.affine_select(out=identf[:], in_=identf[:], pattern=[[-1, 128]], compare_op=ALU.is_equal,
                                           fill=0.0, base=0, channel_multiplier=1), [identf], [identf])
    P.copy("dve", ident[:], identf[:], [identf], [ident])
    P.op("pool", lambda e: e.memset(onesf[:], 1.0), [], [onesf])
    P.copy("dve", ones[:], onesf[:], [onesf], [ones])
    P.ts("dve", ones64m[:], onesf[0:64, 0:64], 1.0 / 64, ALU.mult, R=[onesf], W=[ones64m])
    P.op("pool", lambda e: e.memset(maskA[:], 1.0), [], [maskA])
    P.op("pool", lambda e: e.memset(maskL[:], 1.0), [], [maskL])
    P.op("pool", lambda e: e.affine_select(out=maskA[:, 0:128], in_=maskA[:, 0:128], pattern=[[1, 128]], compare_op=ALU.is_gt,
                                           fill=0.0, base=0, channel_multiplier=-1), [maskA], [maskA])
    P.op("pool", lambda e: e.affine_select(out=maskA[:, 128:256], in_=maskA[:, 128:256], pattern=[[1, 128]], compare_op=ALU.is_ge,
                                           fill=0.0, base=0, channel_multiplier=-1), [maskA], [maskA])
    P.op("pool", lambda e: e.affine_select(out=maskL[:], in_=maskL[:], pattern=[[-1, 128]], compare_op=ALU.is_gt,
                                           fill=0.0, base=0, channel_multiplier=1), [maskL], [maskL])
    P.dma("sp", lk[:], lk_in[:, :], lk, writes=[lk])
    LINK = lk[:, 0:1]; OML = lk[:, 1:2]; NEGM = lk[:, 2:3]

    P.push()
    rbx = P.sbuf("rbx", [33, 8], F32); ohs = P.sbuf("ohs", [33, 768], F32)
    bts = P.sbuf("bts", [8, 768], F32); tz = P.sbuf("tz", [128, 8, 512], F32)
    P.op("pool", lambda e: e.memset(rbx[:], -30000.0), [], [rbx])
    P.dma("sp", rbx[0:32, :], relb[:, :], rbx, writes=[rbx])
    P.dma("sp", ohs[:], oh_in[:, :], ohs, writes=[ohs])
    for c in range(2):
        pb = PS.f32()
        P.mm(pb[0:8, 0:384], rbx[:, :], ohs[:, c * 384:(c + 1) * 384], R=[rbx, ohs], W=[pb])
        P.copy("act", bts[:, c * 384:(c + 1) * 384], pb[0:8, 0:384], [pb], [bts])
    P.dma("sp", btab[:, :], bts[:], bts, reads=[bts], writes=[btab])
    for h in range(8):
        src = bass.AP(tensor=btab.t.tensor, offset=h * 768 + 128, ap=[[1, 128], [1, 512]])
        P.dma("sp", tz[:, h, :], src, tz, reads=[btab], writes=[tz])
    for h in range(8):
        for j in (-1, 0, 1):
            hi = 256 + 128 * j
            P.copy("dve", biasT[:, h, (j + 1) * 128:(j + 2) * 128], tz[:, h, hi:hi - 128:-1], [tz], [biasT])
    P.pop()

    P.push()
    s32 = [P.sbuf("s32", [128, 2048], F32) for _ in range(3)]
    s16 = [P.sbuf("s16", [128, 2048], BF16) for _ in range(3)]
    ci = 0
    for l in range(NL):
        for (src, dst, R_, C_) in ((w_in, wb_in, D, DIN), (w_out, wb_out, D, D), (w_up, wb_up, D, DFF), (w_dn, wb_dn, DFF, D)):
            for r0 in range(0, R_, 128):
                for c0 in range(0, C_, 2048):
                    cw = min(2048, C_ - c0)
                    a = s32[ci % 3]; b = s16[ci % 3]
                    P.dma("sp", a[:, 0:cw], src[l, r0:r0 + 128, c0:c0 + cw], a, writes=[a])
                    P.copy(("act", "dve", "pool")[ci % 3], b[:, 0:cw], a[:, 0:cw], [a], [b])
                    P.dma("sp" if ci % 2 else "pool", dst[l][r0:r0 + 128, c0:c0 + cw], b[:, 0:cw], b, reads=[b])
                    ci += 1
    P.pop()

    def rstd_from_ssq(ssq, tmp, rstd, n):
        P.ts("pool", tmp, ssq, 1.0 / n, ALU.mult, EPS, ALU.add, R=[SM], W=[SM])
        P.act(tmp, tmp, AF.Sqrt, R=[SM], W=[SM])
        P.op("dve", lambda e: e.reciprocal(out=rstd, in_=tmp), [SM], [SM])

    SM = P.sbuf("small", [128, 16], F32)
    CONSTS.append(SM)

    xcur = x_in
    for l in range(nl):
        xnext = xs1 if l < nl - 1 else y_out
        P.push()
        MTA = min(512, SEG)
        NSUB = MTA // 128
        xa = [P.sbuf("xa", [128, D], F32) for _ in range(2)]
        xn = [P.sbuf("xn", [128, D], BF16) for _ in range(2)]
        hT = [P.sbuf("hT", [128, 16, MTA], BF16) for _ in range(2)]
        wb = [P.sbuf("wbA", [128, 16, 640], BF16) for _ in range(2)]
        stf = [P.sbuf("stf", [128, 512], F32) for _ in range(4)]
        stb = [P.sbuf("stb", [128, 512], BF16) for _ in range(4)]
        gt1 = P.sbuf("gt1", [128, 512], F32); gt2 = P.sbuf("gt2", [128, 512], F32)
        gcol = P.sbuf("gcol", [128, 16], F32)
        P.dma("sp", gcol[:], gpre1[l, :, :], gcol, writes=[gcol])
        wv = wb_in[l].t.rearrange("(k p) c -> p k c", p=128)
        si = 0; wi = 0
        for m in range(T // MTA):
            h = hT[m % 2]
            for sub in range(NSUB):
                a = xa[sub % 2]; b = xn[sub % 2]
                t0 = m * MTA + sub * 128
                P.dma("sp", a[:], xcur[t0:t0 + 128, :], a, writes=[a])
                P.act(b[:], a[:], AF.Square, R=[a], W=[b])
                P.op("dve", lambda e, junk=b: e.reduce_sum(out=SM[:, 0:1], in_=junk[:], axis=mybir.AxisListType.X), [b], [SM])
                rstd_from_ssq(SM[:, 0:1], SM[:, 1:2], SM[:, 2:3], D)
                P.ts("dve", b[:], a[:], SM[:, 2:3], ALU.mult, R=[a, SM], W=[b])
                for kq in range(4):
                    pt = PS.bf()
                    for kk in range(4):
                        k = kq * 4 + kk
                        P.tr(pt[:, kk * 128:(kk + 1) * 128], b[:, k * 128:(k + 1) * 128], ident[:], [b, ident], [pt])
                    for kk in range(4):
                        k = kq * 4 + kk
                        eng = "dve" if kk % 2 else "pool"
                        if eng == "pool":
                            P.act(h[:, k, sub * 128:(sub + 1) * 128], pt[:, kk * 128:(kk + 1) * 128], AF.Copy, scale=gcol[:, k:k + 1], R=[pt, gcol], W=[h])
                        else:
                            P.ts("dve", h[:, k, sub * 128:(sub + 1) * 128], pt[:, kk * 128:(kk + 1) * 128], gcol[:, k:k + 1], ALU.mult, R=[pt, gcol], W=[h])
            for cg in range(7):
                w = wb[wi % 2]; wi += 1
                P.dma("sp", w[:], wv[:, :, cg * 640:(cg + 1) * 640], w, writes=[w])
                for cc in range(5):
                    c = cg * 5 + cc
                    if c >= 33:
                        continue
                    ps = PS.f32()
                    for k in range(16):
                        P.mm(ps[:, 0:MTA], w[:, k, cc * 128:(cc + 1) * 128], h[:, k, :], start=(k == 0), stop=(k == 15), R=[w, h], W=[ps])
                    tsl = slice(m * MTA, (m + 1) * MTA)
                    sf = stf[si % 4]; sb_ = stb[si % 4]; si += 1
                    if c < 4:
                        P.copy("act", sf[:, 0:MTA], ps[:, 0:MTA], [ps], [sf])
                        P.dma("pool", lrux[c * 128:(c + 1) * 128, tsl], sf[:, 0:MTA], sf, reads=[sf])
                    elif c < 8:
                        P.act(gt1[:, 0:MTA], ps[:, 0:MTA], AF.Square, R=[ps], W=[gt1])
                        P.ts("dve", gt1[:, 0:MTA], gt1[:, 0:MTA], 2 * GC2, ALU.mult, 2 * GC1, ALU.add, R=[gt1], W=[gt1])
                        P.tt("dve", gt2[:, 0:MTA], gt1[:, 0:MTA], ps[:, 0:MTA], ALU.mult, R=[gt1, ps], W=[gt2])
                        P.act(gt2[:, 0:MTA], gt2[:, 0:MTA], AF.Sigmoid, R=[gt2], W=[gt2])
                        P.tt("dve", sb_[:, 0:MTA], gt2[:, 0:MTA], ps[:, 0:MTA], ALU.mult, R=[gt2, ps], W=[sb_])
                        P.dma("pool", lrug[(c - 4) * 128:(c - 3) * 128, tsl], sb_[:, 0:MTA], sb_, reads=[sb_])
                    elif c < 23:
                        P.copy("act" if c % 2 else "dve", sf[:, 0:MTA], ps[:, 0:MTA], [ps], [sf])
                        P.dma("pool", zsc[(c - 8) * 128:(c - 7) * 128, tsl], sf[:, 0:MTA], sf, reads=[sf])
                    elif c < 31:
                        P.act(sb_[:, 0:MTA], ps[:, 0:MTA], AF.Copy, scale=128.0 ** -0.5, R=[ps], W=[sb_])
                        P.dma("pool", qT[(c - 23) * 128:(c - 22) * 128, tsl], sb_[:, 0:MTA], sb_, reads=[sb_])
                    else:
                        P.copy("dve", sb_[:, 0:MTA], ps[:, 0:MTA], [ps], [sb_])
                        P.dma("pool", kT[(c - 31) * 128:(c - 30) * 128, tsl], sb_[:, 0:MTA], sb_, reads=[sb_])
                if cg == 6:
                    for sub in range(NSUB):
                        ps = PS.f32()
                        for k in range(16):
                            P.mm(ps[:, 0:256], h[:, k, sub * 128:(sub + 1) * 128], w[:, k, 384:640], start=(k == 0), stop=(k == 15), R=[w, h], W=[ps])
                        sb_ = stb[si % 4]; si += 1
                        P.copy("act", sb_[:, 0:256], ps[:, 0:256], [ps], [sb_])
                        t0 = m * MTA + sub * 128
                        P.dma("pool", vtm[t0:t0 + 128, :], sb_[:, 0:256], sb_, reads=[sb_])
        P.pop()

        P.push()
        lc = P.sbuf("lc", [128, 48], F32)
        cc_ = P.sbuf("cc", [128, 16], F32)
        bdf = P.sbuf("bdf", [128, 16, 128], F32); bdb = P.sbuf("bdb", [128, 16, 128], BF16)
        P.dma("sp", lc[:], lru_cols[l, :, :], lc, writes=[lc])
        P.dma("sp", bdf[:], lru_bd[l].rearrange("g p c -> p g c"), bdf, writes=[bdf])
        P.copy("dve", bdb[:], bdf[:], [bdf], [bdb])
        nw = P.sbuf("nw", [128, 48], F32)
        zc = nw[:, 0:8]; tcur = nw[:, 8:16]; th = nw[:, 16:24]; num = nw[:, 24:32]; dn = nw[:, 32:40]
        P.act(zc, lc[:, 36:44], AF.Exp, scale=-1.0, R=[lc], W=[nw])
        P.ts("pool", dn, zc, 2.0, ALU.add, R=[nw], W=[nw])
        P.op("dve", lambda e: e.reciprocal(out=dn, in_=dn), [nw], [nw])
        P.tt("pool", zc, zc, dn, ALU.mult, R=[nw], W=[nw])
        P.copy("dve", tcur, zc, [nw], [nw])
        for _ in range(4):
            P.act(th, tcur, AF.Tanh, R=[nw], W=[nw])
            P.tt("pool", num, th, zc, ALU.subtract, R=[nw], W=[nw])
            P.tt("dve", dn, th, th, ALU.mult, R=[nw], W=[nw])
            P.ts("pool", dn, dn, -1.0, ALU.mult, 1.0, ALU.add, R=[nw], W=[nw])
            P.op("dve", lambda e: e.reciprocal(out=dn, in_=dn), [nw], [nw])
            P.tt("pool", num, num, dn, ALU.mult, R=[nw], W=[nw])
            P.tt("dve", tcur, tcur, num, ALU.subtract, R=[nw], W=[nw])
        P.ts("pool", cc_[:, 8:16], tcur, -32.0, ALU.mult, R=[nw], W=[cc_])
        P.ts("pool", cc_[:, 0:8], tcur, -16.0, ALU.mult, R=[nw], W=[cc_])
        xp = P.sbuf("xp", [128, SEG + 3], F32); xc = P.sbuf("xc", [128, SEG], F32); xcb = P.sbuf("xcb", [128, SEG], BF16)
        rt = P.sbuf("rt", [128, SEG], F32); it = P.sbuf("it", [128, SEG], F32); at = P.sbuf("at", [128, SEG], F32)
        mt = P.sbuf("mt", [128, SEG], F32); ht = P.sbuf("ht", [128, SEG], F32); hfb = P.sbuf("hfb", [128, SEG], F32)
        ggb = P.sbuf("ggb", [128, SEG], BF16); ob = P.sbuf("ob", [128, SEG], BF16)
        carry = P.sbuf("carry", [128, 4], F32)
        CW = min(512, SEG)
        for d in range(2):
            segs = range(4) if d == 0 else range(3, -1, -1)
            for si_, seg in enumerate(segs):
                for ct in range(4):
                    rows = slice(ct * 128, (ct + 1) * 128)
                    s0 = seg * SEG
                    P.dma("sp", xp[:, 2:2 + SEG], lrux[rows, s0:s0 + SEG], xp, writes=[xp])
                    if seg > 0:
                        P.dma("sp", xp[:, 0:2], lrux[rows, s0 - 2:s0], xp, writes=[xp])
                        P.ts("dve", xp[:, 0:2], xp[:, 0:2], LINK, ALU.mult, R=[xp, lk], W=[xp], tiny=True)
                    else:
                        P.op("dve", lambda e: e.memset(xp[:, 0:2], 0.0), [], [xp], tiny=True)
                    if seg < 3:
                        P.dma("sp", xp[:, SEG + 2:SEG + 3], lrux[rows, s0 + SEG:s0 + SEG + 1], xp, writes=[xp], allow_slow_non_contiguous=True)
                        P.ts("dve", xp[:, SEG + 2:SEG + 3], xp[:, SEG + 2:SEG + 3], LINK, ALU.mult, R=[xp, lk], W=[xp], tiny=True)
                    else:
                        P.op("dve", lambda e: e.memset(xp[:, SEG + 2:SEG + 3], 0.0), [], [xp], tiny=True)
                    cw = lambda tap: lc[:, ct * 4 + tap:ct * 4 + tap + 1]
                    P.ts("dve", xc[:], xp[:, 0:SEG], cw(0), ALU.mult, lc[:, 16 + ct:17 + ct], ALU.add, R=[xp, lc], W=[xc])
                    for tap in (1, 2, 3):
                        P.stt(xc[:], xp[:, tap:tap + SEG], cw(tap), xc[:], ALU.mult, ALU.add, R=[xp, lc, xc], W=[xc])
                    P.copy("pool", xcb[:], xc[:], [xc], [xcb])
                    for gate, dst in ((0, rt), (1, it)):
                        gi = gate * 8 + d * 4 + ct
                        bcol = lc[:, 20 + gate * 8 + d * 4 + ct:21 + gate * 8 + d * 4 + ct]
                        for c0 in range(0, SEG, CW):
                            ps = PS.f32()
                            P.mm(ps[:, 0:CW], bdb[:, gi, :], xcb[:, c0:c0 + CW], R=[bdb, xcb], W=[ps])
                            P.act(dst[:, c0:c0 + CW], ps[:, 0:CW], AF.Sigmoid, bias=bcol, R=[ps, lc], W=[dst])
                    ccol = cc_[:, d * 4 + ct:d * 4 + ct + 1]; c2col = cc_[:, 8 + d * 4 + ct:9 + d * 4 + ct]
                    P.act(at[:], rt[:], AF.Exp, scale=ccol, R=[rt, cc_], W=[at])
                    P.act(mt[:], rt[:], AF.Exp, scale=c2col, R=[rt, cc_], W=[mt])
                    P.act(hfb[:], rt[:], AF.Tanh, scale=ccol, R=[rt, cc_], W=[hfb])
                    P.stt(mt[:], mt[:], 1.0, hfb[:], ALU.add, ALU.mult, R=[mt, hfb], W=[mt])
                    P.act(mt[:], mt[:], AF.Sqrt, scale=-1.0, R=[mt], W=[mt])
                    fc = 0 if d == 0 else SEG - 1
                    inner = (seg > 0) if d == 0 else (seg < 3)
                    if inner:
                        P.ts("dve", mt[:, fc:fc + 1], mt[:, fc:fc + 1], LINK, ALU.mult, OML, ALU.add, R=[mt, lk], W=[mt], tiny=True)
                    else:
                        P.op("dve", lambda e: e.memset(mt[:, fc:fc + 1], 1.0), [], [mt], tiny=True)
                    P.tt("dve", it[:], it[:], mt[:], ALU.mult, R=[it, mt], W=[it])
                    P.tt("pool", it[:], it[:], xc[:], ALU.mult, R=[it, xc], W=[it])
                    init = carry[:, ct:ct + 1] if inner else 0.0
                    if d == 0:
                        P.op("dve", lambda e, init=init: e.tensor_tensor_scan(out=ht[:], data0=at[:], data1=it[:], initial=init, op0=ALU.mult, op1=ALU.add), [at, it, carry], [ht])
                        P.ts("dve", carry[:, ct:ct + 1], ht[:, SEG - 1:SEG], LINK, ALU.mult, R=[ht, lk], W=[carry])
                        P.dma("pool", hfs[rows, s0:s0 + SEG], ht[:], ht, reads=[ht])
                    else:
                        P.op("dve", lambda e, init=init: e.tensor_tensor_scan(out=ht[:, ::-1], data0=at[:, ::-1], data1=it[:, ::-1], initial=init, op0=ALU.mult, op1=ALU.add), [at, it, carry], [ht])
                        P.ts("dve", carry[:, ct:ct + 1], ht[:, 0:1], LINK, ALU.mult, R=[ht, lk], W=[carry])
                        P.dma("sp", hfb[:], hfs[rows, s0:s0 + SEG], hfb, reads=[hfs], writes=[hfb])
                        P.dma("sp", ggb[:], lrug[rows, s0:s0 + SEG], ggb, reads=[lrug], writes=[ggb])
                        P.tt("pool", ht[:], ht[:], hfb[:], ALU.add, R=[ht, hfb], W=[ht])
                        P.tt("dve", ob[:], ht[:], ggb[:], ALU.mult, R=[ht, ggb], W=[ob])
                        P.dma("pool", mixT[rows, s0:s0 + SEG], ob[:], ob, reads=[ob])
            if d == 0:
                P.barrier()
        P.pop()

        P.push()
        SL = min(512, SEG)
        NCH = SL // 128
        rc = P.sbuf("rc", [64, 80], F32); rmu = P.sbuf("rmu", [64, 30], F32); omka = P.sbuf("omka", [64, 16], F32)
        P.dma("sp", rc[:], rk_cols[l, :, :], rc, writes=[rc])
        P.dma("sp", rmu[:], rk_mu[l, :, :], rmu, writes=[rmu])
        P.ts("dve", omka[:], rc[:, 32:48], -1.0, ALU.mult, 1.0, ALU.add, R=[rc], W=[omka])
        w2f = P.sbuf("w2f", [64, 2, 512], F32); w2b = P.sbuf("w2b", [64, 2, 512], BF16)
        a2f = P.sbuf("a2f", [64, 2, 512], F32); a2b = P.sbuf("a2b", [64, 2, 512], BF16)
        g2f = P.sbuf("g2f", [64, 2, 512], F32); g2b = P.sbuf("g2b", [64, 2, 512], BF16)
        P.dma("sp", w2f[:], rk_w2[l].rearrange("d r c -> r d c"), w2f, writes=[w2f])
        P.dma("sp", a2f[:], rk_a2[l].rearrange("d r c -> r d c"), a2f, writes=[a2f])
        P.dma("sp", g2f[:], rk_g2[l, :, :, :], g2f, writes=[g2f])
        P.copy("dve", w2b[:], w2f[:], [w2f], [w2b]); P.copy("dve", a2b[:], a2f[:], [a2f], [a2b]); P.copy("dve", g2b[:], g2f[:], [g2f], [g2b])
        NLANE = 3

        def mk_lane():
            zp = [P.sbuf("zp", [64, SL + 2], F32) for _ in range(2)]
            tmp1 = P.sbuf("tmp1", [64, SL], F32); tmp2 = P.sbuf("tmp2", [64, SL], F32)
            zr = P.sbuf("zr", [64, SL], F32); zk = P.sbuf("zk", [64, SL], F32); zv = P.sbuf("zv", [64, SL], F32)
            vb = P.sbuf("vb", [64, SL], BF16)
            logw = P.sbuf("logw", [64, SL], F32); a_t = P.sbuf("a_t", [64, SL], F32); a_o = P.sbuf("a_o", [64, SL], F32)
            kap = P.sbuf("kap", [64, SL], F32); kd = P.sbuf("kd", [64, SL], F32); bb = P.sbuf("bb", [64, SL], F32)
            sqb = P.sbuf("sqb", [64, SL], BF16)
            bon = P.sbuf("bon", [64, SL], F32); g_t = P.sbuf("g_t", [64, SL], F32)
            YT = P.sbuf("YT", [64, SL], F32); yfh = P.sbuf("yfh", [64, SL], F32); orw = P.sbuf("orw", [64, SL], BF16)
            Lc = P.sbuf("Lc", [64, 128], F32); Lx = P.sbuf("Lx", [64, 128], F32)
            Ep = P.sbuf("Ep", [64, 128], F32); Em = P.sbuf("Em", [64, 128], F32); Ex = P.sbuf("Ex", [64, 128], F32)
            Ea = P.sbuf("Ea", [64, 128], F32); Exa = P.sbuf("Exa", [64, 128], F32)
            scol = P.sbuf("scol", [64, 4], F32)
            KR = P.sbuf("KR", [64, 256], BF16); Kt = P.sbuf("Kt", [64, 128], BF16); Bt = P.sbuf("Bt", [64, 128], BF16)
            F5 = P.sbuf("F5", [64, 5, 128], BF16)
            TM = P.sbuf("TM", [128, 5, 64], BF16)
            NBm = P.sbuf("NBm", [128, 256], BF16); NKm = P.sbuf("NKm", [128, 256], BF16); N1 = P.sbuf("N1", [128, 128], BF16)
            PP = [P.sbuf("PP", [128, 256], BF16) for _ in range(2)]
            X32 = P.sbuf("X32", [128, 128], F32); Xb = P.sbuf("Xb", [128, 128], BF16); Xn = P.sbuf("Xn", [128, 128], BF16)
            RhT = P.sbuf("RhT", [64, 128], BF16); AT = P.sbuf("AT", [64, 64], BF16)
            return dict(zp=zp, tmp1=tmp1, tmp2=tmp2, zr=zr, zk=zk, zv=zv, vb=vb, logw=logw, a_t=a_t, a_o=a_o, kap=kap, kd=kd, bb=bb, sqb=sqb, bon=bon, g_t=g_t, YT=YT, yfh=yfh, orw=orw, Lc=Lc, Lx=Lx, Ep=Ep, Em=Em, Ex=Ex, Ea=Ea, Exa=Exa, scol=scol, KR=KR, Kt=Kt, Bt=Bt, F5=F5, TM=TM, NBm=NBm, NKm=NKm, N1=N1, PP=PP, X32=X32, Xb=Xb, Xn=Xn, RhT=RhT, AT=AT)

        LANES = [mk_lane() for _ in range(NLANE)]
        BPREP = dict(zp=[P.sbuf("zpp", [64, SL + 2], F32) for _ in range(2)], tmp1=P.sbuf("tmp1p", [64, SL], F32), tmp2=P.sbuf("tmp2p", [64, SL], F32))
        zlo2 = [[P.sbuf("zlo", [64, SL], BF16) for _ in range(5)] for _ in range(2)]
        Sb = [P.sbuf("Sb%d" % h, [64, 64], BF16) for h in range(8)]

        def load_unit(B, u, d, t0, dst, func=None):
            zp = B["zp"]; tmp1 = B["tmp1"]; tmp2 = B["tmp2"]
            z = zp[u % 2]
            r0 = 64 * u
            P.dma("sp", z[:, 1:SL + 1], zsc[r0:r0 + 64, t0:t0 + SL], z, reads=[zsc], writes=[z])
            for (col, tt_, cond) in ((0, t0 - 1, t0 > 0), (SL + 1, t0 + SL, t0 + SL < T)):
                if cond:
                    P.dma("sp", z[:, col:col + 1], zsc[r0:r0 + 64, tt_:tt_ + 1], z, reads=[zsc], writes=[z], allow_slow_non_contiguous=True)
                    bnd = (t0 % SEG == 0) if col == 0 else ((t0 + SL) % SEG == 0)
                    if bnd:
                        P.ts("dve", z[:, col:col + 1], z[:, col:col + 1], lk[0:64, 0:1], ALU.mult, R=[z, lk], W=[z], tiny=True)
                else:
                    P.op("dve", lambda e, col=col: e.memset(z[:, col:col + 1], 0.0), [], [z], tiny=True)
            if d == 0:
                lo, mid, hi = z[:, 0:SL], z[:, 1:SL + 1], z[:, 2:SL + 2]
            else:
                lo, mid, hi = z[:, SL - 1::-1], z[:, SL:0:-1], z[:, SL + 1:1:-1]
            P.tt("pool", tmp1[:], lo, hi, ALU.add, R=[z], W=[tmp1])
            P.stt(tmp1[:], tmp1[:], 0.5, mid, ALU.mult, ALU.subtract, R=[tmp1, z], W=[tmp1])
            if func is None:
                P.stt(dst[:], tmp1[:], rmu[:, u:u + 1], mid, ALU.mult, ALU.add, R=[tmp1, z, rmu], W=[dst])
            else:
                P.stt(tmp2[:], tmp1[:], rmu[:, u:u + 1], mid, ALU.mult, ALU.add, R=[tmp1, z, rmu], W=[tmp2])
                P.act(dst[:], tmp2[:], func, R=[tmp2], W=[dst])

        def head_work(B, PSL, d, sec, sidx, h, zlo):
            t0 = sec * SL
            zp = B["zp"]
            tmp1 = B["tmp1"]
            tmp2 = B["tmp2"]
            zr = B["zr"]
            zk = B["zk"]
            zv = B["zv"]
            vb = B["vb"]
            logw = B["logw"]
            a_t = B["a_t"]
            a_o = B["a_o"]
            kap = B["kap"]
            kd = B["kd"]
            bb = B["bb"]
            sqb = B["sqb"]
            bon = B["bon"]
            g_t = B["g_t"]
            YT = B["YT"]
            yfh = B["yfh"]
            orw = B["orw"]
            Lc = B["Lc"]
            Lx = B["Lx"]
            Ep = B["Ep"]
            Em = B["Em"]
            Ex = B["Ex"]
            Ea = B["Ea"]
            Exa = B["Exa"]
            scol = B["scol"]
            KR = B["KR"]
            Kt = B["Kt"]
            Bt = B["Bt"]
            F5 = B["F5"]
            TM = B["TM"]
            NBm = B["NBm"]
            NKm = B["NKm"]
            N1 = B["N1"]
            PP = B["PP"]
            X32 = B["X32"]
            Xb = B["Xb"]
            Xn = B["Xn"]
            RhT = B["RhT"]
            AT = B["AT"]
            load_unit(B, h, d, t0, zr)
            load_unit(B, 8 + h, d, t0, zk)
            load_unit(B, 16 + h, d, t0, zv)
            P.copy("pool", vb[:], zv[:], [zv], [vb])
            hc = slice(h * 64, (h + 1) * 64)
            col = lambda base, dd=None: rc[:, base + (h if dd is None else dd * 8 + h):base + (h if dd is None else dd * 8 + h) + 1]
            ps = PSL.f32()
            P.mm(ps[0:64, 0:SL], w2b[:, d, hc], zlo[0][:], R=[w2b, zlo[0]], W=[ps])
            P.act(logw[:], ps[0:64, 0:SL], AF.Sigmoid, bias=col(0, d), R=[ps, rc], W=[logw])
            P.ts("pool", logw[:], logw[:], -DECAY_SCALE, ALU.mult, R=[logw], W=[logw])
            ps = PSL.f32()
            P.mm(ps[0:64, 0:SL], a2b[:, d, hc], zlo[1][:], R=[a2b, zlo[1]], W=[ps])
            P.act(a_t[:], ps[0:64, 0:SL], AF.Sigmoid, bias=col(16, d), R=[ps, rc], W=[a_t])
            P.ts("dve", kap[:], zk[:], col(48), ALU.mult, R=[zk, rc], W=[kap])
            P.act(sqb[:], kap[:], AF.Square, R=[kap], W=[sqb])
            ps = PSL.f32()
            P.mm(ps[0:64, 0:SL], ones[0:64, 0:64], sqb[:], R=[ones, sqb], W=[ps])
            P.ts("dve", tmp2[:], ps[0:64, 0:SL], 1e-24, ALU.max, R=[ps], W=[tmp2])
            P.act(tmp2[:], tmp2[:], AF.Ln, R=[tmp2], W=[tmp2])
            P.act(tmp2[:], tmp2[:], AF.Exp, scale=-0.5, R=[tmp2], W=[tmp2])
            P.tt("dve", kap[:], kap[:], tmp2[:], ALU.mult, R=[kap, tmp2], W=[kap])
            P.ts("dve", tmp2[:], a_t[:], col(32, d), ALU.mult, omka[:, d * 8 + h:d * 8 + h + 1], ALU.add, R=[a_t, rc, omka], W=[tmp2])
            P.tt("dve", kd[:], tmp2[:], zk[:], ALU.mult, R=[tmp2, zk], W=[kd])
            P.tt("pool", bb[:], kap[:], a_t[:], ALU.mult, R=[kap, a_t], W=[bb])
            if d == 1:
                ps = PSL.f32()
                P.mm(ps[0:64, 0:SL], a2b[:, 0, hc], zlo[2][:], R=[a2b, zlo[2]], W=[ps])
                P.act(a_o[:], ps[0:64, 0:SL], AF.Sigmoid, bias=col(16, 0), R=[ps, rc], W=[a_o])
                P.ts("dve", a_o[:], a_o[:], col(32, 0), ALU.mult, omka[:, h:h + 1], ALU.add, R=[a_o, rc, omka], W=[a_o])
                P.tt("dve", a_o[:], a_o[:], tmp2[:], ALU.add, R=[a_o, tmp2], W=[a_o])
                P.tt("dve", a_o[:], a_o[:], zk[:], ALU.mult, R=[a_o, zk], W=[a_o])
                P.stt(sqb[:], a_o[:], col(72), zr[:], ALU.mult, ALU.mult, R=[a_o, rc, zr], W=[sqb])
                ps = PSL.f32()
                P.mm(ps[0:64, 0:SL], ones[0:64, 0:64], sqb[:], R=[ones, sqb], W=[ps])
                P.tt("dve", bon[:], ps[0:64, 0:SL], zv[:], ALU.mult, R=[ps, zv], W=[bon])
                ps = PSL.f32()
                P.mm(ps[0:64, 0:SL], g2b[:, 0, hc], zlo[3][:], start=True, stop=False, R=[g2b, zlo[3]], W=[ps])
                P.mm(ps[0:64, 0:SL], g2b[:, 1, hc], zlo[4][:], start=False, stop=True, R=[g2b, zlo[4]], W=[ps])
                P.copy("act", g_t[:], ps[0:64, 0:SL], [ps], [g_t])
            S = Sb[h]
            for c in range(NCH):
                cs = slice(c * 128, (c + 1) * 128)
                gstart = (t0 + c * 128) if d == 0 else (t0 + SL - c * 128)
                if sidx == 0 and c == 0:
                    P.op("dve", lambda e, S=S: e.memset(S[:], 0.0), [], [S])
                elif gstart % SEG == 0:
                    P.ts("dve", S[:], S[:], lk[0:64, 0:1], ALU.mult, R=[S, lk], W=[S])
                P.op("dve", lambda e, cs=cs: e.tensor_tensor_scan(out=Lc[:], data0=onesf[0:64, 0:128], data1=logw[:, cs], initial=0.0, op0=ALU.mult, op1=ALU.add), [onesf, logw], [Lc])
                P.ts("dve", scol[:, 0:1], Lc[:, 63:64], -1.0, ALU.mult, R=[Lc], W=[scol])
                P.copy("dve", scol[:, 1:2], Lc[:, 63:64], [Lc], [scol])
                P.tt("pool", Lx[:], Lc[:], logw[:, cs], ALU.subtract, R=[Lc, logw], W=[Lx])
                P.act(Ep[:], Lc[:], AF.Exp, bias=scol[:, 0:1], R=[Lc, scol], W=[Ep])
                P.act(Em[:], Lc[:], AF.Exp, bias=scol[:, 1:2], scale=-1.0, R=[Lc, scol], W=[Em])
                P.act(Ex[:], Lx[:], AF.Exp, bias=scol[:, 0:1], R=[Lx, scol], W=[Ex])
                P.act(Ea[:], Lc[:], AF.Exp, R=[Lc], W=[Ea])
                P.act(Exa[:], Lx[:], AF.Exp, R=[Lx], W=[Exa])
                P.tt("dve", KR[:, 0:128], kap[:, cs], Ex[:], ALU.mult, R=[kap, Ex], W=[KR])
                P.tt("pool", KR[:, 128:256], zr[:, cs], Ep[:], ALU.mult, R=[zr, Ep], W=[KR])
                P.tt("dve", Kt[:], kd[:, cs], Em[:], ALU.mult, R=[kd, Em], W=[Kt])
                P.tt("pool", Bt[:], bb[:, cs], Em[:], ALU.mult, R=[bb, Em], W=[Bt])
                P.copy("pool", F5[:, 0, :], vb[:, cs], [vb], [F5])
                P.ts("dve", F5[:, 1, :], Kt[:], Ep[:, 127:128], ALU.mult, R=[Kt, Ep], W=[F5])
                P.ts("dve", F5[:, 2, :], Bt[:], Ep[:, 127:128], ALU.mult, R=[Bt, Ep], W=[F5])
                P.tt("pool", F5[:, 3, :], kap[:, cs], Exa[:], ALU.mult, R=[kap, Exa], W=[F5])
                P.tt("dve", F5[:, 4, :], zr[:, cs], Ea[:], ALU.mult, R=[zr, Ea], W=[F5])
                pt = PSL.bf()
                for i in range(5):
                    P.tr(pt[:, i * 64:(i + 1) * 64], F5[:, i, :], ident[0:64, 0:64], [F5, ident], [pt])
                P.copy("act", TM[:].rearrange("p a b -> p (a b)"), pt[:, 0:320], [pt], [TM])
                Vt = TM[:, 0, :]; Kbt = TM[:, 1, :]; Bbt = TM[:, 2, :]; kabt = TM[:, 3, :]; Rabt = TM[:, 4, :]
                pa = PSL.f32()
                P.mm(pa[:, 0:256], Bt[:], KR[:], R=[Bt, KR], W=[pa])
                P.tt("dve", NBm[:], pa[:, 0:256], maskA[:], ALU.mult, R=[pa, maskA], W=[NBm])
                pb = PSL.f32()
                P.mm(pb[:, 0:256], Kt[:], KR[:], R=[Kt, KR], W=[pb])
                P.tt("dve", NKm[:], pb[:, 0:256], maskA[:], ALU.mult, R=[pb, maskA], W=[NKm])
                pc = PSL.f32()
                P.mm(pc[:, 0:128], KR[:, 0:128], Bt[:], R=[Bt, KR], W=[pc])
                P.tt("dve", N1[:], pc[:, 0:128], maskL[:], ALU.mult, R=[pc, maskL], W=[N1])
                px = PSL.f32()
                P.mm(px[:, 0:64], NKm[:, 0:128], Vt, R=[NKm, TM], W=[px])
                P.copy("pool", X32[:, 0:64], kabt, [TM], [X32])
                P.copy("act", X32[:, 64:128], px[:, 0:64], [px], [X32])
                P.copy("pool", Xb[:], X32[:], [X32], [Xb])
                py = PSL.f32()
                P.mm(py[:, 0:128], NBm[:, 0:128], Xb[:], R=[NBm, Xb], W=[py])
                P.tt("dve", X32[:], X32[:], py[:, 0:128], ALU.subtract, R=[X32, py], W=[X32])
                P.copy("pool", Xb[:], X32[:], [X32], [Xb])
                Pm, PTm, Pb_, PTb = N1[:], NBm[:, 0:128], N1, NBm
                for lvl in range(6):
                    pp = PP[lvl % 2]
                    psq = PSL.f32()
                    P.mm(psq[:, 0:128], PTm, Pm, R=[Pb_, PTb], W=[psq])
                    P.mm(psq[:, 128:256], Pm, PTm, R=[Pb_, PTb], W=[psq])
                    P.copy("act", pp[:], psq[:, 0:256], [psq], [pp])
                    Pm, PTm, Pb_, PTb = pp[:, 0:128], pp[:, 128:256], pp, pp
                    py = PSL.f32()
                    P.mm(py[:, 0:128], PTm, Xb[:], R=[pp, Xb], W=[py])
                    P.tt("dve", X32[:], X32[:], py[:, 0:128], ALU.add, R=[X32, py], W=[X32])
                    if lvl < 5:
                        P.copy("pool", Xb[:], X32[:], [X32], [Xb])
                    else:
                        P.ts("dve", Xn[:], X32[:], -1.0, ALU.mult, R=[X32], W=[Xn])
                nkab = Xn[:, 0:64]; nU = Xn[:, 64:128]
                pr = PSL.f32()
                P.mm(pr[0:64, 0:128], Rabt, ident[:], start=True, stop=False, R=[TM, ident], W=[pr])
                P.mm(pr[0:64, 0:128], nkab, NBm[:, 128:256], start=False, stop=True, R=[Xn, NBm], W=[pr])
                P.copy("act", RhT[:], pr[0:64, 0:128], [pr], [RhT])
                pat = PSL.f32()
                P.mm(pat[0:64, 0:64], nkab, Bbt, R=[Xn, TM], W=[pat])
                P.stt(AT[:], identf[0:64, 0:64], Ea[:, 127:128], pat[0:64, 0:64], ALU.mult, ALU.add, R=[identf, Ea, pat], W=[AT])
                pyl = PSL.f32()
                P.mm(pyl[0:64, 0:128], Vt, NKm[:, 128:256], start=True, stop=False, R=[TM, NKm], W=[pyl])
                P.mm(pyl[0:64, 0:128], nU, NBm[:, 128:256], start=False, stop=False, R=[Xn, NBm], W=[pyl])
                P.mm(pyl[0:64, 0:128], S[:], RhT[:], start=False, stop=True, R=[S, RhT], W=[pyl])
                P.copy("act", YT[:, cs], pyl[0:64, 0:128], [pyl], [YT])
                pg = PSL.f32()
                P.mm(pg[0:64, 0:64], Kbt, Vt, start=True, stop=False, R=[TM], W=[pg])
                P.mm(pg[0:64, 0:64], Bbt, nU, start=False, stop=False, R=[TM, Xn], W=[pg])
                P.mm(pg[0:64, 0:64], AT[:], S[:], start=False, stop=True, R=[AT, S], W=[pg])
                P.copy("dve", S[:], pg[0:64, 0:64], [pg], [S])
            rows = slice(h * 64, (h + 1) * 64)
            if d == 0:
                P.dma("pool", yfs[rows, t0:t0 + SL], YT[:], YT, reads=[YT])
            else:
                P.dma("sp", yfh[:], yfs[rows, t0:t0 + SL], yfh, reads=[yfs], writes=[yfh])
                P.tt("dve", YT[:], YT[:], yfh[:, ::-1], ALU.add, R=[YT, yfh], W=[YT])
                P.copy("pool", sqb[:], YT[:], [YT], [sqb])
                ps = PSL.f32()
                P.mm(ps[0:64, 0:SL], ones64m[:], sqb[:], R=[ones64m, sqb], W=[ps])
                P.tt("dve", YT[:], YT[:], ps[0:64, 0:SL], ALU.subtract, R=[YT, ps], W=[YT])
                P.act(sqb[:], YT[:], AF.Square, R=[YT], W=[sqb])
                ps = PSL.f32()
                P.mm(ps[0:64, 0:SL], ones64m[:], sqb[:], R=[ones64m, sqb], W=[ps])
                P.ts("dve", tmp2[:], ps[0:64, 0:SL], GN_EPS, ALU.add, R=[ps], W=[tmp2])
                P.act(tmp2[:], tmp2[:], AF.Ln, R=[tmp2], W=[tmp2])
                P.act(tmp2[:], tmp2[:], AF.Exp, scale=-0.5, R=[tmp2], W=[tmp2])
                P.tt("dve", YT[:], YT[:], tmp2[:], ALU.mult, R=[YT, tmp2], W=[YT])
                P.ts("dve", YT[:], YT[:], col(56), ALU.mult, col(64), ALU.add, R=[YT, rc], W=[YT])
                P.tt("pool", YT[:], YT[:], bon[:], ALU.add, R=[YT, bon], W=[YT])
                P.tt("dve", orw[:, ::-1], YT[:], g_t[:], ALU.mult, R=[YT, g_t], W=[orw])
                P.dma("pool", mixT[512 + h * 64:512 + (h + 1) * 64, t0:t0 + SL], orw[:], orw, reads=[orw])


        class LanePS:
            def __init__(self, i):
                self.f = [PS.f[2 * i], PS.f[min(2 * i + 1, 4)]]; self.b = PS.b[i]; self.k = 0
            def f32(self):
                self.k += 1; return self.f[self.k % 2]
            def bf(self):
                return self.b
        LPS = [LanePS(i) for i in range(NLANE)]

        for d in range(2):
            secs = list(range(T // SL)) if d == 0 else list(range(T // SL - 1, -1, -1))
            items = []
            for sidx, sec in enumerate(secs):
                items.append(("prep", sidx, sec, None))
                for h in range(8):
                    items.append(("head", sidx, sec, h))
            items.reverse()
            lane_ops = [[] for _ in range(NLANE)]
            lane_pos = [0] * NLANE
            inflight_secs = [None] * NLANE
            def refill(i):
                while items:
                    kind, sidx, sec, h = items[-1]
                    if kind == "prep":
                        if any(inflight_secs[j] is not None and inflight_secs[j] <= sidx - 2 and lane_pos[j] < len(lane_ops[j]) for j in range(NLANE)):
                            return False
                        items.pop()
                        zl = zlo2[sidx % 2]
                        t0 = sec * SL
                        load_unit(BPREP, 24 + d, d, t0, zl[0], func=AF.Tanh)
                        load_unit(BPREP, 26 + d, d, t0, zl[1], func=AF.Copy)
                        if d == 1:
                            load_unit(BPREP, 26, d, t0, zl[2], func=AF.Copy)
                            load_unit(BPREP, 28, d, t0, zl[3], func=AF.Sigmoid)
                            load_unit(BPREP, 29, d, t0, zl[4], func=AF.Sigmoid)
                        continue
                    items.pop()
                    P.rec_begin()
                    head_work(LANES[i], LPS[i], d, sec, sidx, h, zlo2[sidx % 2])
                    lane_ops[i] = P.rec_end(); lane_pos[i] = 0; inflight_secs[i] = sidx
                    return True
                return False
            active = True
            while active:
                active = False
                for i in range(NLANE):
                    if lane_pos[i] >= len(lane_ops[i]):
                        refill(i)
                    if lane_pos[i] < len(lane_ops[i]):
                        P.replay(lane_ops[i][lane_pos[i]]); lane_pos[i] += 1
                        active = True
                if not active and items:
                    active = any(refill(i) for i in range(NLANE))
            if d == 0:
                P.barrier()
        P.pop()

        P.push()
        SA = min(512, SEG)
        NQ = SA // 128
        Qs = P.sbuf("Qs", [128, 8, SA], BF16); Ks = P.sbuf("Ks", [128, 2, SA + 256], BF16)
        Vs = P.sbuf("Vs", [128, NQ + 2, 256], BF16); OT = P.sbuf("OT", [128, 8, SA], BF16)
        Sbt = [P.sbuf("Sbt", [128, 384], F32) for _ in range(2)]
        PT = [P.sbuf("PT", [128, 3, 4, 128], BF16) for _ in range(2)]
        den = P.sbuf("den", [128, 512], F32)
        esk = P.sbuf("esk", [128, 8], F32)
        P.dma("sp", esk[:], sink_in[l, :, :], esk, writes=[esk])
        P.act(esk[:], esk[:], AF.Exp, R=[esk], W=[esk])
        qv = qT.t.rearrange("(h p) t -> p h t", p=128)
        kv = kT.t.rearrange("(g p) t -> p g t", p=128)
        vv = vtm.t.rearrange("(b p) c -> p b c", p=128)
        mv = mixT.t.rearrange("(c p) t -> p c t", p=128)
        pi = 0
        for sec in range(T // SA):
            t0 = sec * SA
            P.dma("sp", Qs[:], qv[:, :, t0:t0 + SA], Qs, reads=[qT], writes=[Qs])
            lo = max(0, t0 - 128); hi = min(T, t0 + SA + 128)
            P.dma("sp", Ks[:, :, lo - (t0 - 128):hi - (t0 - 128)], kv[:, :, lo:hi], Ks, reads=[kT], writes=[Ks])
            P.dma("sp", Vs[:, (lo - (t0 - 128)) // 128:(hi - (t0 - 128)) // 128, :], vv[:, lo // 128:hi // 128, :], Vs, reads=[vtm], writes=[Vs])
            for i in range(NQ):
                n = sec * NQ + i
                js = [j for j in (-1, 0, 1) if 0 <= n + j < NB]
                jlo = (js[0] + 1) * 128; jhi = (js[-1] + 2) * 128
                for g in range(2):
                    pt = PT[pi % 2]; pi += 1
                    for h in range(4):
                        hh = 4 * g + h
                        ps = PS.f32()
                        for j in js:
                            P.mm(ps[:, (j + 1) * 128:(j + 2) * 128], Ks[:, g, (i + j + 1) * 128:(i + j + 2) * 128], Qs[:, hh, i * 128:(i + 1) * 128], R=[Ks, Qs], W=[ps])
                        sb_ = Sbt[h % 2]
                        P.tt("dve", sb_[:, jlo:jhi], ps[:, jlo:jhi], biasT[:, hh, jlo:jhi], ALU.add, R=[ps, biasT], W=[sb_])
                        for j in js:
                            cross = ((n + j) // NBSEG) != (n // NBSEG)
                            if cross:
                                P.act(pt[:, j + 1, h, :], sb_[:, (j + 1) * 128:(j + 2) * 128], AF.Exp, bias=NEGM, R=[sb_, lk], W=[pt])
                            else:
                                P.act(pt[:, j + 1, h, :], sb_[:, (j + 1) * 128:(j + 2) * 128], AF.Exp, R=[sb_], W=[pt])
                    pd = PS.f32()
                    for j in js:
                        P.mm(pd[:, :], ones[:], pt[:, j + 1, :, :].rearrange("p a b -> p (a b)"), start=(j == js[0]), stop=(j == js[-1]), R=[ones, pt], W=[pd])
                    po = PS.f32()
                    for h in range(4):
                        for j in js:
                            P.mm(po[:, h * 128:(h + 1) * 128], Vs[:, i + j + 1, g * 128:(g + 1) * 128], pt[:, j + 1, h, :], start=(j == js[0]), stop=(j == js[-1]), R=[Vs, pt], W=[po])
                    for h in range(4):
                        P.ts("dve", den[:, h * 128:(h + 1) * 128], pd[:, h * 128:(h + 1) * 128], esk[:, 4 * g + h:4 * g + h + 1], ALU.add, R=[pd, esk], W=[den])
                    P.op("dve", lambda e: e.reciprocal(out=den[:], in_=den[:]), [den], [den])
                    for h in range(4):
                        P.tt("dve", OT[:, 4 * g + h, i * 128:(i + 1) * 128], po[:, h * 128:(h + 1) * 128], den[:, h * 128:(h + 1) * 128], ALU.mult, R=[po, den], W=[OT])
            P.dma("pool", mv[:, 8:16, t0:t0 + SA], OT[:], OT, reads=[OT])
        P.pop()

        P.push()
        ME = 256
        mixs = P.sbuf("mixs", [128, 16, ME], BF16)
        wbe = [P.sbuf("wbe", [128, 16, 512], BF16) for _ in range(3)]
        osb = P.sbuf("osb", [128, 2, D], F32); fsb = P.sbuf("fsb", [128, 2, D], F32)
        xin = [P.sbuf("xin", [128, D], F32) for _ in range(2)]
        xn2 = [P.sbuf("xn2", [128, D], BF16) for _ in range(2)]
        h2T = P.sbuf("h2T", [128, 16, ME], BF16); hid = P.sbuf("hid", [128, 64, ME], BF16)
        gB1 = P.sbuf("gB1", [128, D], F32); gB2 = P.sbuf("gB2", [128, D], F32); gc2 = P.sbuf("gc2", [128, 16], F32)
        sqt = P.sbuf("sqt", [128, ME], F32)
        P.dma("sp", gB1[:], gpost1[l, :, :], gB1, writes=[gB1])
        P.dma("sp", gB2[:], gpost2[l, :, :], gB2, writes=[gB2])
        P.dma("sp", gc2[:], gpre2[l, :, :], gc2, writes=[gc2])
        wo = wb_out[l].t.rearrange("(k p) c -> p k c", p=128)
        wu = wb_up[l].t.rearrange("(k p) c -> p k c", p=128)
        wd = wb_dn[l].t.rearrange("(f p) c -> p f c", p=128)
        mk = mixT.t.rearrange("(k p) t -> p k t", p=128)
        wi = 0
        if dbg and l == nl - 1:
            dbg_o = P.dram("dbg_o", [T, D], F32); dbg_x1 = P.dram("dbg_x1", [T, D], F32); dbg_f = P.dram("dbg_f", [T, D], F32)
            dbg_hid = P.dram("dbg_hid", [DFF, T], BF16)
        for m in range(T // ME):
            t0 = m * ME
            P.dma("sp", mixs[:], mk[:, :, t0:t0 + ME], mixs, reads=[mixT], writes=[mixs])
            for n4 in range(4):
                w = wbe[wi % 3]; wi += 1
                P.dma("sp", w[:], wo[:, :, n4 * 512:(n4 + 1) * 512], w, writes=[w])
                for sub in range(2):
                    ps = PS.f32()
                    for k in range(16):
                        P.mm(ps[:, :], mixs[:, k, sub * 128:(sub + 1) * 128], w[:, k, :], start=(k == 0), stop=(k == 15), R=[mixs, w], W=[ps])
                    P.copy("act" if sub else "dve", osb[:, sub, n4 * 512:(n4 + 1) * 512], ps[:, :], [ps], [osb])
            for sub in range(2):
                xi = xin[sub]; xb_ = xn2[sub]
                if dbg and l == nl - 1:
                    P.dma("pool", dbg_o[t0 + sub * 128:t0 + (sub + 1) * 128, :], osb[:, sub, :], osb, reads=[osb])
                P.dma("sp", xi[:], xcur[t0 + sub * 128:t0 + (sub + 1) * 128, :], xi, writes=[xi])
                P.act(xb_[:], osb[:, sub, :], AF.Square, R=[osb], W=[xb_])
                P.op("dve", lambda e, junk=xb_: e.reduce_sum(out=SM[:, 4:5], in_=junk[:], axis=mybir.AxisListType.X), [xb_], [SM])
                rstd_from_ssq(SM[:, 4:5], SM[:, 5:6], SM[:, 6:7], D)
                P.stt(osb[:, sub, :], osb[:, sub, :], SM[:, 6:7], gB1[:], ALU.mult, ALU.mult, R=[osb, SM, gB1], W=[osb])
                P.tt("pool", osb[:, sub, :], osb[:, sub, :], xi[:], ALU.add, R=[osb, xi], W=[osb])
                if dbg and l == nl - 1:
                    P.dma("pool", dbg_x1[t0 + sub * 128:t0 + (sub + 1) * 128, :], osb[:, sub, :], osb, reads=[osb])
                P.act(xb_[:], osb[:, sub, :], AF.Square, R=[osb], W=[xb_])
                P.op("dve", lambda e, junk=xb_: e.reduce_sum(out=SM[:, 7:8], in_=junk[:], axis=mybir.AxisListType.X), [xb_], [SM])
                rstd_from_ssq(SM[:, 7:8], SM[:, 8:9], SM[:, 9:10], D)
                P.ts("dve", xb_[:], osb[:, sub, :], SM[:, 9:10], ALU.mult, R=[osb, SM], W=[xb_])
                for kq in range(4):
                    pt = PS.bf()
                    for kk in range(4):
                        k = kq * 4 + kk
                        P.tr(pt[:, kk * 128:(kk + 1) * 128], xb_[:, k * 128:(k + 1) * 128], ident[:], [xb_, ident], [pt])
                    for kk in range(4):
                        k = kq * 4 + kk
                        if kk % 2:
                            P.act(h2T[:, k, sub * 128:(sub + 1) * 128], pt[:, kk * 128:(kk + 1) * 128], AF.Copy, scale=gc2[:, k:k + 1], R=[pt, gc2], W=[h2T])
                        else:
                            P.ts("dve", h2T[:, k, sub * 128:(sub + 1) * 128], pt[:, kk * 128:(kk + 1) * 128], gc2[:, k:k + 1], ALU.mult, R=[pt, gc2], W=[h2T])
            for fg in range(16):
                w = wbe[wi % 3]; wi += 1
                P.dma("sp", w[:], wu[:, :, fg * 512:(fg + 1) * 512], w, writes=[w])
                for fcc in range(4):
                    fc = fg * 4 + fcc
                    ps = PS.f32()
                    for k in range(16):
                        P.mm(ps[:, 0:ME], w[:, k, fcc * 128:(fcc + 1) * 128], h2T[:, k, :], start=(k == 0), stop=(k == 15), R=[w, h2T], W=[ps])
                    P.act(sqt[:], ps[:, 0:ME], AF.Square, R=[ps], W=[sqt])
                    P.stt(hid[:, fc, :], ps[:, 0:ME], 0.0, sqt[:], ALU.is_gt, ALU.mult, R=[ps, sqt], W=[hid])
            for n4 in range(4):
                pacc = [PS.f32(), PS.f32()]
                for kq in range(4):
                    w = wbe[wi % 3]; wi += 1
                    P.dma("sp", w[:], wd[:, kq * 16:(kq + 1) * 16, n4 * 512:(n4 + 1) * 512], w, writes=[w])
                    for sub in range(2):
                        for kk in range(16):
                            P.mm(pacc[sub][:, :], hid[:, kq * 16 + kk, sub * 128:(sub + 1) * 128], w[:, kk, :], start=(kq == 0 and kk == 0), stop=(kq == 3 and kk == 15), R=[hid, w], W=[pacc[sub]])
                for sub in range(2):
                    P.copy("act" if sub else "dve", fsb[:, sub, n4 * 512:(n4 + 1) * 512], pacc[sub][:, :], [pacc[sub]], [fsb])
            if dbg and l == nl - 1:
                P.dma("pool", dbg_hid.t.rearrange("(f p) t -> p f t", p=128)[:, :, t0:t0 + ME], hid[:], hid, reads=[hid])
            for sub in range(2):
                xb_ = xn2[sub]
                if dbg and l == nl - 1:
                    P.dma("pool", dbg_f[t0 + sub * 128:t0 + (sub + 1) * 128, :], fsb[:, sub, :], fsb, reads=[fsb])
                P.act(xb_[:], fsb[:, sub, :], AF.Square, R=[fsb], W=[xb_])
                P.op("dve", lambda e, junk=xb_: e.reduce_sum(out=SM[:, 10:11], in_=junk[:], axis=mybir.AxisListType.X), [xb_], [SM])
                rstd_from_ssq(SM[:, 10:11], SM[:, 11:12], SM[:, 12:13], D)
                P.stt(fsb[:, sub, :], fsb[:, sub, :], SM[:, 12:13], gB2[:], ALU.mult, ALU.mult, R=[fsb, SM, gB2], W=[fsb])
                P.tt("pool", fsb[:, sub, :], fsb[:, sub, :], osb[:, sub, :], ALU.add, R=[fsb, osb], W=[fsb])
                P.dma("pool", xnext[t0 + sub * 128:t0 + (sub + 1) * 128, :], fsb[:, sub, :], fsb, reads=[fsb])
        P.pop()
        xcur = xnext
    P.barrier()
    es.close()
    return nc


def _t5_bucket_np(rel):
    half = 16; max_exact = 8
    ret = np.where(rel > 0, half, 0)
    n = np.abs(rel)
    nf = np.maximum(n, 1).astype(np.float32)
    large = max_exact + (np.log(nf / max_exact) / math.log(128 / max_exact) * (half - max_exact)).astype(np.int32)
    large = np.minimum(large, half - 1)
    return ret + np.where(n < max_exact, n, large)


def host_layout(inp):
    f = lambda a: np.ascontiguousarray(a, dtype=np.float32)
    L = NL
    m = {}
    m["w_in"] = f(inp["w_in"]); m["w_out"] = f(inp["w_out"]); m["w_up"] = f(inp["w_up"]); m["w_down"] = f(inp["w_down"])
    m["gpre1"] = f(inp["norm_mix_pre"].reshape(L, 16, 128).transpose(0, 2, 1))
    m["gpre2"] = f(inp["norm_ffn_pre"].reshape(L, 16, 128).transpose(0, 2, 1))
    m["gpost1"] = f(np.broadcast_to(inp["norm_mix_post"][:, None, :], (L, 128, D)))
    m["gpost2"] = f(np.broadcast_to(inp["norm_ffn_post"][:, None, :], (L, 128, D)))
    lc = np.zeros((L, 128, 48), np.float32)
    lc[:, :, 0:16] = inp["conv_w"].reshape(L, 4, 4, 128).transpose(0, 3, 2, 1).reshape(L, 128, 16)
    lc[:, :, 16:20] = inp["conv_b"].reshape(L, 4, 128).transpose(0, 2, 1)
    lc[:, :, 20:28] = inp["lru_ba"].reshape(L, 2, 4, 128).transpose(0, 3, 1, 2).reshape(L, 128, 8)
    lc[:, :, 28:36] = inp["lru_bx"].reshape(L, 2, 4, 128).transpose(0, 3, 1, 2).reshape(L, 128, 8)
    lc[:, :, 36:44] = inp["lru_lambda"].reshape(L, 2, 4, 128).transpose(0, 3, 1, 2).reshape(L, 128, 8)
    m["lru_cols"] = lc
    bd = np.zeros((L, 2, 2, 4, 128, 128), np.float32)
    for gi, nm in enumerate(("lru_wa", "lru_wx")):
        w = inp[nm]
        for ct in range(4):
            bd[:, gi, :, ct, 0:64, 0:64] = w[:, :, 2 * ct]
            bd[:, gi, :, ct, 64:128, 64:128] = w[:, :, 2 * ct + 1]
    m["lru_bd"] = bd.reshape(L, 16, 128, 128)
    m["rk_mu"] = f(inp["rwkv_mu"].reshape(L, 30, 64).transpose(0, 2, 1))
    rc = np.zeros((L, 64, 80), np.float32)
    dh = lambda a: a.reshape(L, 2, 8, 64).transpose(0, 3, 1, 2).reshape(L, 64, 16)
    hh = lambda a: a.reshape(L, 8, 64).transpose(0, 2, 1)
    rc[:, :, 0:16] = dh(inp["rwkv_w0"]); rc[:, :, 16:32] = dh(inp["rwkv_a0"]); rc[:, :, 32:48] = dh(inp["rwkv_k_a"])
    rc[:, :, 48:56] = hh(inp["rwkv_k_k"]); rc[:, :, 56:64] = hh(inp["rwkv_gn_w"]); rc[:, :, 64:72] = hh(inp["rwkv_gn_b"])
    rc[:, :, 72:80] = hh(inp["rwkv_r_k"])
    m["rk_cols"] = rc
    m["rk_w2"] = f(inp["rwkv_w2"]); m["rk_a2"] = f(inp["rwkv_a2"])
    m["rk_g2"] = f(inp["rwkv_g2"].reshape(L, 2, 64, 512).transpose(0, 2, 1, 3))
    m["sink"] = f(np.broadcast_to(inp["attn_sink"][:, None, :], (L, 128, 8)))
    m["relb"] = f(inp["rel_bias"])
    rel = np.arange(768) - 384
    oh = np.zeros((33, 768), np.float32)
    bk = _t5_bucket_np(rel)
    valid = np.abs(rel) <= 128
    oh[bk[valid], np.nonzero(valid)[0]] = 1.0
    oh[32, ~valid] = 1.0
    m["oh"] = oh
    return m


def _lk(link):
    a = np.zeros((128, 4), np.float32)
    a[:, 0] = link; a[:, 1] = 1.0 - link; a[:, 2] = (link - 1.0) * 30000.0
    return a


_NC_CACHE = {}


def kernel(**inputs):
    inp = {k: np.asarray(v) for k, v in inputs.items()}
    xp = inp["x_prompt"].astype(np.float32); xs = inp["x_sample"].astype(np.float32)
    T = 8192
    m = host_layout(inp)
    in_maps = []
    for c in range(4):
        d = dict(m); d["x"] = np.ascontiguousarray(xs[c]); d["lk"] = _lk(1.0); in_maps.append(d)
    for c in range(4):
        d = dict(m)
        d["x"] = np.ascontiguousarray(np.concatenate([xp[2 * c], xp[2 * c + 1], xp[2 * c], xp[2 * c + 1]], axis=0))
        d["lk"] = _lk(0.0); in_maps.append(d)
    nc = build(T)
    res = run_bass_kernel_spmd(nc, in_maps, core_ids=list(range(8)))
    ys = np.stack([res.results[c]["y"] for c in range(4)], axis=0).astype(np.float32)
    yp = np.empty_like(xp)
    for c in range(4):
        y = res.results[4 + c]["y"]
        yp[2 * c] = y[0:2048]; yp[2 * c + 1] = y[2048:4096]
    return (yp, ys)
```
